# Optimizing a Trainium2 kernel written in Bass

```python
import math
import jax, jax.numpy as jnp
from jax import lax
import numpy as np

D_MODEL = 1024
BATCH = 8
SEQ = 4096
DEPTH = 4

N_EVEN = (DEPTH + 1) // 2
N_ODD = DEPTH // 2
QBLK = 128
EPS = 1e-6
MASK_VALUE = -1e30

A_WIDTH = D_MODEL // 2
A_HEAD_DIM = 64
A_HEADS = A_WIDTH // (2 * A_HEAD_DIM)
B_WIDTH = D_MODEL - A_WIDTH
POOL_WINDOWS = (2, 4, 8, 16)
B_GROUPS = len(POOL_WINDOWS)
B_GROUP_DIM = B_WIDTH // B_GROUPS
AB_IN = 3 * A_WIDTH + B_WIDTH
C_HEAD_DIM = 64
C_Q_HEADS = D_MODEL // C_HEAD_DIM
C_KV_HEADS = 2
C_GROUP = C_Q_HEADS // C_KV_HEADS
C_WINDOW = 128
C_IN = (C_Q_HEADS + 2 * C_KV_HEADS) * C_HEAD_DIM
D_FF = 2816
CONV_W = 3

kernel_name = "hybrid_diffattn_pool_swa_sink_convffn"


def rms_norm(x, g):
    xf = x.astype(jnp.float32)
    y = xf * lax.rsqrt(jnp.mean(xf * xf, axis=-1, keepdims=True) + EPS)
    return (y * g.astype(jnp.float32)).astype(x.dtype)


def alibi_slopes(n):
    return 2.0 ** (-8.0 * jnp.arange(1, n + 1, dtype=jnp.float32) / n)


def diff_attention(q, k, v, lam, lam_init, sub_g):
    B, S = q.shape[0], q.shape[1]
    nb = S // QBLK
    scale = A_HEAD_DIM ** -0.5
    slopes = alibi_slopes(A_HEADS)
    kf = k.astype(jnp.float32)
    vf = v.astype(jnp.float32)
    key_pos = jnp.arange(S)
    qb = q.reshape(B, nb, QBLK, A_HEADS, 2, A_HEAD_DIM).transpose(1, 0, 2, 3, 4, 5)

    def block(args):
        qi, i = args
        qpos = i * QBLK + jnp.arange(QBLK)
        rel = (qpos[:, None] - key_pos[None, :]).astype(jnp.float32)
        bias = jnp.where(rel[None] >= 0, -slopes[:, None, None] * rel[None], MASK_VALUE)
        s = jnp.einsum('bqhmd,bkhmd->bhmqk', qi.astype(jnp.float32), kf) * scale + bias[None, :, None]
        p = jax.nn.softmax(s, axis=-1)
        attn = p[:, :, 0] - lam * p[:, :, 1]
        return jnp.einsum('bhqk,bkhe->bqhe', attn, vf)

    o = lax.map(block, (qb, jnp.arange(nb)))
    o = o.transpose(1, 0, 2, 3, 4).reshape(B, S, A_HEADS, 2 * A_HEAD_DIM)
    o = rms_norm(o, sub_g) * (1.0 - lam_init)
    return o.reshape(B, S, A_WIDTH)


def pool_mixer(u, w_group, scale):
    B, S = u.shape[0], u.shape[1]
    uf = u.astype(jnp.float32)
    cs = jnp.pad(lax.cumsum(uf, axis=1), ((0, 0), (1, 0), (0, 0)))
    pos = jnp.arange(S)
    outs = []
    for g, w in enumerate(POOL_WINDOWS):
        lo_c, hi_c = g * B_GROUP_DIM, (g + 1) * B_GROUP_DIM
        csg = cs[:, :, lo_c:hi_c]
        start = jnp.maximum(pos + 1 - w, 0)
        win_sum = csg[:, 1:] - csg[:, start]
        cnt = jnp.minimum(pos + 1, w).astype(jnp.float32)
        outs.append(win_sum / cnt[None, :, None] - uf[:, :, lo_c:hi_c])
    pooled = jnp.stack(outs, axis=2)
    mixed = jnp.einsum('bsgc,gcd->bsgd', pooled, w_group.astype(jnp.float32))
    return mixed.reshape(B, S, B_WIDTH) * scale.astype(jnp.float32)


def sliding_window_attention(q, k, v, sinks):
    B, S = q.shape[0], q.shape[1]
    nb = S // QBLK
    scale = C_HEAD_DIM ** -0.5
    slopes = alibi_slopes(C_Q_HEADS).reshape(C_KV_HEADS, C_GROUP)
    sink = sinks.astype(jnp.float32).reshape(C_KV_HEADS, C_GROUP)[None, :, :, None, None]
    pad = ((0, 0), (QBLK, 0), (0, 0), (0, 0))
    kp = jnp.pad(k.astype(jnp.float32), pad)
    vp = jnp.pad(v.astype(jnp.float32), pad)
    qb = q.reshape(B, nb, QBLK, C_KV_HEADS, C_GROUP, C_HEAD_DIM).transpose(1, 0, 2, 3, 4, 5)
    a_idx = jnp.arange(QBLK)[:, None]
    j_idx = jnp.arange(2 * QBLK)[None, :]
    rel = QBLK + a_idx - j_idx
    alibi = -slopes[:, :, None, None] * rel.astype(jnp.float32)

    def block(args):
        qi, i = args
        kb = lax.dynamic_slice_in_dim(kp, i * QBLK, 2 * QBLK, axis=1)
        vb = lax.dynamic_slice_in_dim(vp, i * QBLK, 2 * QBLK, axis=1)
        key_pos = (i - 1) * QBLK + j_idx
        valid = (rel >= 0) & (rel < C_WINDOW) & (key_pos >= 0)
        bias = jnp.where(valid, alibi, MASK_VALUE)
        s = jnp.einsum('bqhgd,bkhd->bhgqk', qi.astype(jnp.float32), kb) * scale + bias
        m = jnp.maximum(jnp.max(s, axis=-1, keepdims=True), sink)
        p = jnp.exp(s - m)
        denom = jnp.sum(p, axis=-1, keepdims=True) + jnp.exp(sink - m)
        return jnp.einsum('bhgqk,bkhd->bqhgd', p / denom, vb)

    o = lax.map(block, (qb, jnp.arange(nb)))
    return o.transpose(1, 0, 2, 3, 4, 5).reshape(B, S, C_Q_HEADS * C_HEAD_DIM)


def diff_pool_layer(x, layer_idx, norm_g, w_in, q_g, k_g, lam_p, sub_g, w_group, pool_scale, w_out):
    B, S = x.shape[0], x.shape[1]
    proj = rms_norm(x, norm_g) @ w_in
    q, k, v, u = jnp.split(proj, [A_WIDTH, 2 * A_WIDTH, 3 * A_WIDTH], axis=-1)
    q = rms_norm(q.reshape(B, S, A_HEADS, 2, A_HEAD_DIM), q_g)
    k = rms_norm(k.reshape(B, S, A_HEADS, 2, A_HEAD_DIM), k_g)
    v = v.reshape(B, S, A_HEADS, 2 * A_HEAD_DIM)
    lam_init = 0.8 - 0.6 * math.exp(-0.3 * layer_idx)
    lp = lam_p.astype(jnp.float32)
    lam = jnp.exp(jnp.sum(lp[0] * lp[1])) - jnp.exp(jnp.sum(lp[2] * lp[3])) + lam_init
    a_out = diff_attention(q, k, v, lam, lam_init, sub_g)
    b_out = pool_mixer(u, w_group, pool_scale)
    mix = jnp.concatenate([a_out.astype(jnp.float32), b_out], axis=-1).astype(x.dtype)
    return x + mix @ w_out


def swa_layer(x, norm_g, w_in, q_g, k_g, sinks, w_out):
    B, S = x.shape[0], x.shape[1]
    proj = rms_norm(x, norm_g) @ w_in
    kv_w = C_KV_HEADS * C_HEAD_DIM
    q, k, v = jnp.split(proj, [C_Q_HEADS * C_HEAD_DIM, C_Q_HEADS * C_HEAD_DIM + kv_w], axis=-1)
    q = rms_norm(q.reshape(B, S, C_KV_HEADS, C_GROUP, C_HEAD_DIM), q_g)
    k = rms_norm(k.reshape(B, S, C_KV_HEADS, C_HEAD_DIM), k_g)
    v = v.reshape(B, S, C_KV_HEADS, C_HEAD_DIM)
    o = sliding_window_attention(q, k, v, sinks).astype(x.dtype)
    return x + o @ w_out


def conv_glu_ffn(x, norm_g, w_up, conv_w, conv_b, w_down):
    S = x.shape[1]
    h = rms_norm(x, norm_g) @ w_up
    hp = jnp.pad(h, ((0, 0), (CONV_W - 1, 0), (0, 0)))
    c = conv_b
    for j in range(CONV_W):
        c = c + conv_w[j] * hp[:, j:j + S]
    gate, up = jnp.split(c, 2, axis=-1)
    return x + (jax.nn.silu(gate) * up) @ w_down


def setup_inputs(seed: int = 0) -> dict:
    key = jax.random.key(seed)
    ks = jax.random.split(key, 24)
    f32 = jnp.float32
    nrm = lambda k, shape, s: jax.random.normal(k, shape, f32) * s
    return {
        "x": nrm(ks[0], (BATCH, SEQ, D_MODEL), 1.0),
        "ab_norm": 1.0 + nrm(ks[1], (N_EVEN, D_MODEL), 0.02),
        "ab_w_in": nrm(ks[2], (N_EVEN, D_MODEL, AB_IN), D_MODEL ** -0.5),
        "a_q_norm": 1.0 + nrm(ks[3], (N_EVEN, A_HEAD_DIM), 0.02),
        "a_k_norm": 1.0 + nrm(ks[4], (N_EVEN, A_HEAD_DIM), 0.02),
        "a_lambda": nrm(ks[5], (N_EVEN, 4, A_HEAD_DIM), 0.1),
        "a_sub_norm": 1.0 + nrm(ks[6], (N_EVEN, 2 * A_HEAD_DIM), 0.02),
        "b_w_group": nrm(ks[7], (N_EVEN, B_GROUPS, B_GROUP_DIM, B_GROUP_DIM), B_GROUP_DIM ** -0.5),
        "b_scale": 1.0 + nrm(ks[8], (N_EVEN, B_WIDTH), 0.1),
        "ab_w_out": nrm(ks[9], (N_EVEN, A_WIDTH + B_WIDTH, D_MODEL), (A_WIDTH + B_WIDTH) ** -0.5),
        "c_norm": 1.0 + nrm(ks[10], (N_ODD, D_MODEL), 0.02),
        "c_w_in": nrm(ks[11], (N_ODD, D_MODEL, C_IN), D_MODEL ** -0.5),
        "c_q_norm": 1.0 + nrm(ks[12], (N_ODD, C_HEAD_DIM), 0.02),
        "c_k_norm": 1.0 + nrm(ks[13], (N_ODD, C_HEAD_DIM), 0.02),
        "c_sinks": nrm(ks[14], (N_ODD, C_Q_HEADS), 0.5),
        "c_w_out": nrm(ks[15], (N_ODD, C_Q_HEADS * C_HEAD_DIM, D_MODEL), (C_Q_HEADS * C_HEAD_DIM) ** -0.5),
        "f_norm": 1.0 + nrm(ks[16], (DEPTH, D_MODEL), 0.02),
        "f_w_up": nrm(ks[17], (DEPTH, D_MODEL, 2 * D_FF), D_MODEL ** -0.5),
        "f_conv": nrm(ks[18], (DEPTH, CONV_W, 2 * D_FF), CONV_W ** -0.5),
        "f_conv_b": nrm(ks[19], (DEPTH, 2 * D_FF), 0.02),
        "f_w_down": nrm(ks[20], (DEPTH, D_FF, D_MODEL), D_FF ** -0.5),
    }


def reference(x, ab_norm, ab_w_in, a_q_norm, a_k_norm, a_lambda, a_sub_norm, b_w_group, b_scale,
              ab_w_out, c_norm, c_w_in, c_q_norm, c_k_norm, c_sinks, c_w_out,
              f_norm, f_w_up, f_conv, f_conv_b, f_w_down):
    for layer in range(DEPTH):
        if layer % 2 == 0:
            e = layer // 2
            x = diff_pool_layer(x, layer, ab_norm[e], ab_w_in[e], a_q_norm[e], a_k_norm[e],
                                a_lambda[e], a_sub_norm[e], b_w_group[e], b_scale[e], ab_w_out[e])
        else:
            o = layer // 2
            x = swa_layer(x, c_norm[o], c_w_in[o], c_q_norm[o], c_k_norm[o], c_sinks[o], c_w_out[o])
        x = conv_glu_ffn(x, f_norm[layer], f_w_up[layer], f_conv[layer], f_conv_b[layer], f_w_down[layer])
    return x
```

```python
import numpy as np
import math
from contextlib import ExitStack
import concourse.bass as bass
import concourse.mybir as mybir
from concourse.bass_utils import run_bass_kernel_spmd

F32 = mybir.dt.float32
BF16 = mybir.dt.bfloat16
AF = mybir.ActivationFunctionType
ALU = mybir.AluOpType

S = 4096
D = 1024
DFF = 2816
TT = 512
NT = S // TT
EPS = 1e-6
NFP = DFF // 128


class Buf:
    __slots__ = ("name", "w", "r", "dsem")

    def __init__(self, name):
        self.name = name
        self.w = None
        self.r = {}
        self.dsem = None


class Tile:
    def __init__(self, t, buf):
        self.t = t
        self.b = buf

    def __getitem__(self, idx):
        return self.t[idx]


class Plan:
    ENGS = ("sp", "pe", "act", "dve", "pool")

    def __init__(self, nc):
        self.nc = nc
        self.ops = {e: [] for e in self.ENGS}
        self.cnt = {}
        self.waited = {e: {} for e in self.ENGS}
        self.semkeys = []
        for e in self.ENGS:
            self._newsem("e_" + e)
        self.dsem_pool = []
        self.dsem_next = 0
        self.n_ops = 0
        self.sb_off = 0
        self.sb_hi = 0
        self.uid = 0

    def _newsem(self, key):
        self.semkeys.append(key)
        self.cnt[key] = 0

    SB_BASE = 16512
    SB_TOP = 229344

    def arena_reset(self):
        self.sb_off = self.SB_BASE

    def tile(self, name, shape, dtype, nbuf=1):
        esz = 2 if dtype == BF16 else 4
        n = 1
        for s_ in shape[1:]:
            n *= s_
        nbytes = (n * esz + 63) // 64 * 64
        self.uid += 1
        t = self.nc.alloc_sbuf_tensor_at(f"{name}_{self.uid}", list(shape), dtype, offset=self.sb_off)
        self.sb_off += nbytes
        self.sb_hi = max(self.sb_hi, self.sb_off)
        assert self.sb_off <= self.SB_TOP, f"SBUF overflow at {name}: {self.sb_off}"
        return Tile(t, Buf(name))

    def op(self, eng, fn, r=(), w=(), dma=None):
        rb = [x.b if isinstance(x, Tile) else x for x in r]
        wb = [x.b if isinstance(x, Tile) else x for x in w]
        need = {}

        def add(sv):
            if sv is None:
                return
            k, v = sv
            if need.get(k, 0) < v:
                need[k] = v

        for b in rb:
            add(b.w)
        for b in wb:
            add(b.w)
            for k, v in b.r.items():
                add((k, v))
        own = "e_" + eng
        if eng == "pe":
            need.pop(own, None)
        waits = []
        wd = self.waited[eng]
        for k, v in need.items():
            if wd.get(k, 0) < v:
                wd[k] = v
                waits.append((k, v))
        if dma is not None:
            db = dma.b if isinstance(dma, Tile) else dma
            if db.dsem is None:
                if self.dsem_next < len(self.dsem_pool):
                    db.dsem = self.dsem_pool[self.dsem_next]
                else:
                    db.dsem = "d_%d" % len(self.dsem_pool)
                    self.dsem_pool.append(db.dsem)
                    self._newsem(db.dsem)
                self.dsem_next += 1
            key, inc = db.dsem, 16
        else:
            key, inc = own, 1
        self.cnt[key] += inc
        val = self.cnt[key]
        for b in wb:
            b.w = (key, val)
            b.r = {}
        for b in rb:
            if b.r.get(key, 0) < val:
                b.r[key] = val
        self.ops[eng].append((waits, fn, key, inc))
        self.n_ops += 1

    def barrier(self):
        for e in self.ENGS:
            waits = []
            wd = self.waited[e]
            for k in self.semkeys:
                v = self.cnt[k]
                if v > 0 and wd.get(k, 0) < v:
                    wd[k] = v
                    waits.append((k, v))
            if waits:
                self.ops[e].append((waits, None, None, 0))
        self.dsem_next = 0

    def emit(self):
        nc = self.nc
        with ExitStack() as st:
            semh = {}
            for k in self.semkeys:
                semh[k] = st.enter_context(nc.semaphore("s" + str(len(semh))))
            block = st.enter_context(nc.Block())
            decos = {"sp": block.sync, "pe": block.tensor, "act": block.scalar,
                     "dve": block.vector, "pool": block.gpsimd}
            for name in self.ENGS:
                ops = self.ops[name]

                def body(e, ops=ops):
                    for waits, fn, key, inc in ops:
                        for (k, v) in waits:
                            e.wait_ge(semh[k], v)
                        if fn is not None:
                            ins = fn(e)
                            ins.then_inc(semh[key], inc)

                decos[name](body)


class Ctx:
    pass


def rms_prologue(P, C, xa, gn, xb_chunks, sqs, rstd, ps_ssq, ones_bf):
    for c in range(8):
        sq = sqs[c % 2]
        P.op("pool", lambda e, c=c, sq=sq: e.tensor_tensor(out=sq[:, :], in0=xa[:, c, :], in1=xa[:, c, :], op=ALU.mult),
             r=[xa], w=[sq])
        P.op("pe", lambda e, c=c, sq=sq: e.matmul(ps_ssq[:, :], lhsT=ones_bf[:, :], rhs=sq[:, :], start=(c == 0), stop=(c == 7)),
             r=[sq, ones_bf], w=[ps_ssq])
    P.op("act", lambda e: e.activation(out=rstd[:, :], in_=ps_ssq[:, :], func=AF.Ln, scale=1.0 / D, bias=C.eps_col[:, 0:1]),
         r=[ps_ssq, C.eps_col], w=[rstd])
    P.op("act", lambda e: e.activation(out=rstd[:, :], in_=rstd[:, :], func=AF.Exp, scale=-0.5), r=[rstd], w=[rstd])
    for c in range(8):
        P.op("dve", lambda e, c=c: e.scalar_tensor_tensor(out=xb_chunks[c][:, :], in0=xa[:, c, :], scalar=gn[:, c:c + 1],
                                                          in1=rstd[:, :], op0=ALU.mult, op1=ALU.mult),
             r=[xa, gn, rstd], w=[xb_chunks[c]])


def ffn_phase(P, C, l, Xin, Xout):
    nc = P.nc
    P.arena_reset()
    ones_bf = P.tile("ones", [128, 128], BF16)
    wup = [P.tile(f"wup{p}", [128, 8, 256], BF16) for p in range(NFP)]
    wdn = [P.tile(f"wdn{j}", [128, 2, 1024], BF16) for j in range(NFP // 2)]
    cpar = P.tile("cpar", [128, 44 * 4], F32)
    gn = P.tile("gn", [128, 8], F32)
    xa = P.tile("xa", [128, 8, TT], F32)
    xb = [P.tile(f"xb{c}", [128, TT], BF16) for c in range(8)]
    sqs = [P.tile(f"sq{i}", [128, TT], BF16) for i in range(2)]
    rstd = P.tile("rstd", [128, TT], F32)
    cg = [P.tile(f"cg{i}", [128, TT], F32) for i in range(2)]
    cu = [P.tile(f"cu{i}", [128, TT], F32) for i in range(2)]
    sg = [P.tile(f"sg{i}", [128, TT], F32) for i in range(2)]
    act = [P.tile(f"act{f}", [128, TT], BF16) for f in range(NFP)]
    xc = [P.tile(f"xc{i}", [128, TT], F32) for i in range(4)]
    halo = P.tile("halo", [128, 44, 2], F32)
    ps = C.ps

    P.op("pool", lambda e: e.memset(ones_bf[:, :], 1.0), w=[ones_bf])
    P.op("pool", lambda e: e.memset(halo[:, :, :], 0.0), w=[halo])
    P.op("sp", lambda e: e.dma_start(out=cpar[:, :], in_=C.d_fcpar[l]), w=[cpar], dma=cpar)
    P.op("sp", lambda e: e.dma_start(out=gn[:, :], in_=C.d_fnorm[l]), w=[gn], dma=gn)
    P.op("sp", lambda e: e.dma_start(out=xa[:, :, :], in_=Xin[:, :, 0:TT]), w=[xa], dma=xa)
    for p in range(NFP):
        P.op("pool", lambda e, p=p: e.dma_start(out=wup[p][:, :, :], in_=C.d_wup[l, p]), w=[wup[p]], dma=wup[p])
    for j in range(NFP // 2):
        P.op("pool", lambda e, j=j: e.dma_start(out=wdn[j][:, :, :], in_=C.d_wdn[l, j]), w=[wdn[j]], dma=wdn[j])

    rms_prologue(P, C, xa, gn, xb, sqs, rstd, ps[0], ones_bf)

    xci = 0
    for i in range(NT):
        t0 = i * TT
        if i + 1 < NT:
            P.op("sp", lambda e, t1=t0 + TT: e.dma_start(out=xa[:, :, :], in_=Xin[:, :, t1:t1 + TT]), w=[xa], dma=xa)
        for f in range(NFP):
            s = f % 2
            for gu in range(2):
                bank = ps[1 + 2 * s + gu]
                ch = f + gu * NFP
                cdst = (cg if gu == 0 else cu)[s]
                w0 = cpar[:, 4 * ch + 0:4 * ch + 1]
                w1 = cpar[:, 4 * ch + 1:4 * ch + 2]
                w2 = cpar[:, 4 * ch + 2:4 * ch + 3]
                bb = cpar[:, 4 * ch + 3:4 * ch + 4]
                hl = halo

                def mm(e, f=f, gu=gu, bank=bank):
                    ins = None
                    for c in range(8):
                        ins = e.matmul(bank[:, :], lhsT=wup[f][:, c, gu * 128:(gu + 1) * 128], rhs=xb[c][:, :],
                                       start=(c == 0), stop=(c == 7))
                    return ins
                P.op("pe", mm, r=[wup[f]] + xb, w=[bank])
                P.op("act", lambda e, bank=bank, cdst=cdst, w2=w2, bb=bb: e.activation(
                    out=cdst[:, :], in_=bank[:, :], func=AF.Identity, bias=bb, scale=w2), r=[bank, cpar], w=[cdst])
                P.op("dve", lambda e, bank=bank, cdst=cdst, w1=w1: e.scalar_tensor_tensor(
                    out=cdst[:, 1:TT], in0=bank[:, 0:TT - 1], scalar=w1, in1=cdst[:, 1:TT], op0=ALU.mult, op1=ALU.add),
                    r=[bank, cpar, cdst], w=[cdst])
                P.op("dve", lambda e, bank=bank, cdst=cdst, w0=w0: e.scalar_tensor_tensor(
                    out=cdst[:, 2:TT], in0=bank[:, 0:TT - 2], scalar=w0, in1=cdst[:, 2:TT], op0=ALU.mult, op1=ALU.add),
                    r=[bank, cpar, cdst], w=[cdst])
                P.op("dve", lambda e, ch=ch, cdst=cdst, w1=w1: e.scalar_tensor_tensor(
                    out=cdst[:, 0:1], in0=hl[:, ch, 1:2], scalar=w1, in1=cdst[:, 0:1], op0=ALU.mult, op1=ALU.add),
                    r=[hl, cpar, cdst], w=[cdst])
                P.op("dve", lambda e, ch=ch, cdst=cdst, w0=w0: e.scalar_tensor_tensor(
                    out=cdst[:, 0:2], in0=hl[:, ch, 0:2], scalar=w0, in1=cdst[:, 0:2], op0=ALU.mult, op1=ALU.add),
                    r=[hl, cpar, cdst], w=[cdst])
                P.op("dve", lambda e, ch=ch, bank=bank: e.tensor_copy(out=hl[:, ch, :], in_=bank[:, TT - 2:TT]),
                     r=[bank], w=[hl])
            P.op("act", lambda e, s=s: e.activation(out=sg[s][:, :], in_=cg[s][:, :], func=AF.Silu), r=[cg[s]], w=[sg[s]])
            P.op("pool", lambda e, s=s, f=f: e.tensor_tensor(out=act[f][:, :], in0=sg[s][:, :], in1=cu[s][:, :], op=ALU.mult),
                 r=[sg[s], cu[s]], w=[act[f]])
        if i + 1 < NT:
            rms_prologue(P, C, xa, gn, xb, sqs, rstd, ps[0], ones_bf)
        for c in range(8):
            bank = ps[5 + c % 2]
            xt = xc[xci % 4]
            xci += 1
            P.op("sp", lambda e, xt=xt, c=c, t0=t0: e.dma_start(out=xt[:, :], in_=Xin[:, c, t0:t0 + TT]), w=[xt], dma=xt)

            def mm2(e, c=c, bank=bank):
                ins = None
                for f in range(NFP):
                    ins = e.matmul(bank[:, :], lhsT=wdn[f // 2][:, f % 2, c * 128:(c + 1) * 128], rhs=act[f][:, :],
                                   start=(f == 0), stop=(f == NFP - 1))
                return ins
            P.op("pe", mm2, r=wdn + act, w=[bank])
            P.op("dve", lambda e, xt=xt, bank=bank: e.tensor_tensor(out=xt[:, :], in0=bank[:, :], in1=xt[:, :], op=ALU.add),
                 r=[bank, xt], w=[xt])
            P.op("sp", lambda e, xt=xt, c=c, t0=t0: e.dma_start(out=Xout[:, c, t0:t0 + TT], in_=xt[:, :]), r=[xt], dma=xt)
    P.barrier()


A_SLOPES = [2.0 ** (-8.0 * (h + 1) / 4) for h in range(4)]
C_SLOPES = [2.0 ** (-8.0 * (h + 1) / 16) for h in range(16)]
NDEL = 35


def load_x_and_norm_setup(P, C, d_norm_l):
    ones_bf = P.tile("ones", [128, 128], BF16)
    gn = P.tile("gn", [128, 8], F32)
    xa = P.tile("xa", [128, 8, TT], F32)
    xb = [P.tile(f"xb{c}", [128, TT], BF16) for c in range(8)]
    sqs = [P.tile(f"sq{i}", [128, TT], BF16) for i in range(2)]
    rstd = P.tile("rstd", [128, TT], F32)
    P.op("pool", lambda e: e.memset(ones_bf[:, :], 1.0), w=[ones_bf])
    P.op("sp", lambda e: e.dma_start(out=gn[:, :], in_=d_norm_l), w=[gn], dma=gn)
    return ones_bf, gn, xa, xb, sqs, rstd


def qk_norm_chunk(P, C, bank, ssq_bank, bd, gcol, qf, qsq, qr, qn, nfeat):
    P.op("act", lambda e: e.activation(out=qf[:, :], in_=bank[:, :], func=AF.Identity), r=[bank], w=[qf])
    P.op("pool", lambda e: e.tensor_tensor(out=qsq[:, :], in0=qf[:, :], in1=qf[:, :], op=ALU.mult), r=[qf], w=[qsq])
    P.op("pe", lambda e: e.matmul(ssq_bank[:, :], lhsT=bd[:, :], rhs=qsq[:, :], start=True, stop=True),
         r=[bd, qsq], w=[ssq_bank])
    P.op("act", lambda e: e.activation(out=qr[:, :], in_=ssq_bank[:, :], func=AF.Ln, scale=1.0 / nfeat, bias=C.eps_col[:, 0:1]),
         r=[ssq_bank, C.eps_col], w=[qr])
    P.op("act", lambda e: e.activation(out=qr[:, :], in_=qr[:, :], func=AF.Exp, scale=-0.5), r=[qr], w=[qr])
    P.op("dve", lambda e: e.scalar_tensor_tensor(out=qn[:, :], in0=qf[:, :], scalar=gcol, in1=qr[:, :],
                                                 op0=ALU.mult, op1=ALU.mult), r=[qf, qr], w=[qn])


def make_bd(P):
    bd = P.tile("bd", [128, 128], BF16)
    P.op("pool", lambda e: e.memset(bd[:, :], 0.0), w=[bd])
    P.op("pool", lambda e: e.memset(bd[0:64, 0:64], 1.0), w=[bd])
    P.op("pool", lambda e: e.memset(bd[64:128, 64:128], 1.0), w=[bd])
    return bd


def even_proj_phase(P, C, l, Xin):
    e_ = l // 2
    P.arena_reset()
    ps = C.ps
    ones_bf, gn, xa, xb, sqs, rstd = load_x_and_norm_setup(P, C, C.d_abnorm[e_])
    bd = make_bd(P)
    win = [P.tile(f"win{k}", [128, 8, 512], BF16) for k in range(4)]
    wgrp = P.tile("wgrp", [128, 4, 128], BF16)
    gq = P.tile("gq", [128, 2], F32)
    bsc = P.tile("bsc", [128, 4], F32)
    invc = P.tile("invc", [128, 4, TT], F32)
    qf = [P.tile(f"qf{i}", [128, TT], F32) for i in range(2)]
    qsq = [P.tile(f"qsq{i}", [128, TT], BF16) for i in range(2)]
    qr = [P.tile(f"qr{i}", [128, TT], F32) for i in range(2)]
    qn = [P.tile(f"qn{i}", [128, TT], BF16) for i in range(3)]
    vtm = [P.tile(f"vtm{i}", [128, 512], BF16) for i in range(2)]
    ubuf = [P.tile(f"ubuf{g}", [128, TT + 16], F32) for g in range(4)]
    Tt = [P.tile(f"Tt{i}", [128, TT + 16], F32) for i in range(2)]
    pooled = [P.tile(f"pooled{i}", [128, TT], BF16) for i in range(2)]
    bo = [P.tile(f"bo{i}", [128, TT], BF16) for i in range(2)]

    P.op("sp", lambda e: e.dma_start(out=gq[:, :], in_=C.d_aqk[e_]), w=[gq], dma=gq)
    P.op("sp", lambda e: e.dma_start(out=bsc[:, :], in_=C.d_bscale[e_]), w=[bsc], dma=bsc)
    P.op("sp", lambda e: e.dma_start(out=invc[:, :, :], in_=C.d_invc), w=[invc], dma=invc)
    P.op("sp", lambda e: e.dma_start(out=xa[:, :, :], in_=Xin[:, :, 0:TT]), w=[xa], dma=xa)
    for k in range(4):
        P.op("pool", lambda e, k=k: e.dma_start(out=win[k][:, :, :], in_=C.d_abwin[e_, k]), w=[win[k]], dma=win[k])
    P.op("pool", lambda e: e.dma_start(out=wgrp[:, :, :], in_=C.d_wgrp[e_]), w=[wgrp], dma=wgrp)
    P.op("dve", lambda e: e.tensor_scalar(out=gq[:, 0:1], in0=gq[:, 0:1], scalar1=0.125, scalar2=None, op0=ALU.mult),
         r=[gq], w=[gq])
    for g in range(4):
        P.op("pool", lambda e, g=g: e.memset(ubuf[g][:, 0:16], 0.0), w=[ubuf[g]])

    rms_prologue(P, C, xa, gn, xb, sqs, rstd, ps[0], ones_bf)
    qi = 0
    for i in range(NT):
        t0 = i * TT
        if i + 1 < NT:
            P.op("sp", lambda e, t1=t0 + TT: e.dma_start(out=xa[:, :, :], in_=Xin[:, :, t1:t1 + TT]), w=[xa], dma=xa)
        for j in range(8):
            bank = ps[1 + j % 2]
            piece, sub = divmod(j, 4)

            def mm(e, bank=bank, piece=piece, sub=sub):
                ins = None
                for c in range(8):
                    ins = e.matmul(bank[:, :], lhsT=win[piece][:, c, sub * 128:(sub + 1) * 128], rhs=xb[c][:, :],
                                   start=(c == 0), stop=(c == 7))
                return ins
            P.op("pe", mm, r=[win[piece]] + xb, w=[bank])
            s = j % 2
            qo = qn[qi % 3]
            qi += 1
            qk_norm_chunk(P, C, bank, ps[3], bd, gq[:, piece:piece + 1], qf[s], qsq[s], qr[s], qo, 64)
            P.op("sp", lambda e, qo=qo, j=j, t0=t0: e.dma_start(out=C.d_QK[j * 128:(j + 1) * 128, t0:t0 + TT], in_=qo[:, :]),
                 r=[qo], dma=qo)
        for b in range(4):
            bank = ps[4 + b % 2]

            def mmv(e, bank=bank, b=b):
                ins = None
                for c in range(8):
                    ins = e.matmul(bank[:, :], lhsT=xb[c][:, b * 128:(b + 1) * 128], rhs=win[2][:, c, :],
                                   start=(c == 0), stop=(c == 7))
                return ins
            P.op("pe", mmv, r=[win[2]] + xb, w=[bank])
            vt = vtm[b % 2]
            P.op("act", lambda e, vt=vt, bank=bank: e.activation(out=vt[:, :], in_=bank[:, :], func=AF.Identity), r=[bank], w=[vt])
            P.op("sp", lambda e, vt=vt, b=b, t0=t0: e.dma_start(out=C.d_V[t0 + b * 128:t0 + (b + 1) * 128, :], in_=vt[:, :]),
                 r=[vt], dma=vt)
        for g in range(4):
            bank = ps[4 + g % 2]
            wlen = 2 ** (g + 1)

            def mmu(e, bank=bank, g=g):
                ins = None
                for c in range(8):
                    ins = e.matmul(bank[:, :], lhsT=win[3][:, c, g * 128:(g + 1) * 128], rhs=xb[c][:, :],
                                   start=(c == 0), stop=(c == 7))
                return ins
            P.op("pe", mmu, r=[win[3]] + xb, w=[bank])
            ub = ubuf[g]
            P.op("act", lambda e, ub=ub, bank=bank: e.activation(out=ub[:, 16:16 + TT], in_=bank[:, :], func=AF.Identity),
                 r=[bank], w=[ub])
            src = ub
            for k in range(g + 1):
                sh = 2 ** k
                lo = 2 ** (k + 1) - 1
                dst = Tt[k % 2]
                P.op("pool", lambda e, src=src, dst=dst, lo=lo, sh=sh: e.tensor_tensor(
                    out=dst[:, lo:TT + 16], in0=src[:, lo:TT + 16], in1=src[:, lo - sh:TT + 16 - sh], op=ALU.add),
                    r=[src], w=[dst])
                src = dst
            pl = pooled[g % 2]
            if i == 0:
                P.op("pool", lambda e, src=src, g=g: e.tensor_tensor(out=src[:, 16:16 + TT], in0=src[:, 16:16 + TT],
                                                                     in1=invc[:, g, :], op=ALU.mult), r=[src, invc], w=[src])
                sc = 1.0
            else:
                sc = 1.0 / wlen
            P.op("dve", lambda e, src=src, ub=ub, pl=pl, sc=sc: e.scalar_tensor_tensor(
                out=pl[:, :], in0=src[:, 16:16 + TT], scalar=sc, in1=ub[:, 16:16 + TT], op0=ALU.mult, op1=ALU.subtract),
                r=[src, ub], w=[pl])
            P.op("pool", lambda e, ub=ub: e.tensor_copy(out=ub[:, 0:16], in_=ub[:, TT:TT + 16]), r=[ub], w=[ub])
            P.op("pe", lambda e, pl=pl, g=g: e.matmul(ps[6][:, :], lhsT=wgrp[:, g, :], rhs=pl[:, :], start=True, stop=True),
                 r=[wgrp, pl], w=[ps[6]])
            bt = bo[g % 2]
            P.op("act", lambda e, bt=bt, g=g: e.activation(out=bt[:, :], in_=ps[6][:, :], func=AF.Identity, scale=bsc[:, g:g + 1]),
                 r=[ps[6], bsc], w=[bt])
            P.op("sp", lambda e, bt=bt, g=g, t0=t0: e.dma_start(out=C.d_MIXB[g * 128:(g + 1) * 128, t0:t0 + TT], in_=bt[:, :]),
                 r=[bt], dma=bt)
        if i + 1 < NT:
            rms_prologue(P, C, xa, gn, xb, sqs, rstd, ps[0], ones_bf)
    P.barrier()


def out_proj_tile(P, C, wout, rhs_chunks, Xin, Xout, t0, xc, xci, banks):
    for c in range(8):
        bank = banks[c % 2]
        xt = xc[xci % len(xc)]
        xci += 1
        P.op("sp", lambda e, xt=xt, c=c: e.dma_start(out=xt[:, :], in_=Xin[:, c, t0:t0 + TT]), w=[xt], dma=xt)

        def mm(e, c=c, bank=bank):
            ins = None
            for j in range(8):
                ins = e.matmul(bank[:, :], lhsT=wout[:, j, c * 128:(c + 1) * 128], rhs=rhs_chunks[j],
                               start=(j == 0), stop=(j == 7))
            return ins
        P.op("pe", mm, r=[wout] + C._oproj_reads, w=[bank])
        P.op("dve", lambda e, xt=xt, bank=bank: e.tensor_tensor(out=xt[:, :], in0=bank[:, :], in1=xt[:, :], op=ALU.add),
             r=[bank, xt], w=[xt])
        P.op("sp", lambda e, xt=xt, c=c: e.dma_start(out=Xout[:, c, t0:t0 + TT], in_=xt[:, :]), r=[xt], dma=xt)
    return xci


def even_attn_phase(P, C, l, Xin, Xout):
    e_ = l // 2
    lam_init = 0.8 - 0.6 * math.exp(-0.3 * l)
    P.arena_reset()
    ps = C.ps
    ones_bf = P.tile("ones", [128, 128], BF16)
    P.op("pool", lambda e: e.memset(ones_bf[:, :], 1.0), w=[ones_bf])
    kt = [[P.tile(f"kt{h}{m}", [66, S], BF16) for m in range(2)] for h in range(4)]
    vt = P.tile("vt", [128, 32, 512], BF16)
    wout = P.tile("wout", [128, 8, 1024], BF16)
    biasc = P.tile("biasc", [128, 4 * NDEL], F32)
    mask = P.tile("mask", [128, 128], BF16)
    qt = [[[P.tile(f"qt{h}{m}{s}", [66, TT], BF16) for s in range(2)] for m in range(2)] for h in range(4)]
    pt = [P.tile(f"pt{i}", [128, TT], BF16) for i in range(4)]
    r0 = P.tile("r0", [128, TT], F32)
    rl = P.tile("rl", [128, TT], F32)
    comb = P.tile("comb", [128, TT], F32)
    csq = P.tile("csq", [128, TT], BF16)
    cr = P.tile("cr", [128, TT], F32)
    aout = [P.tile(f"aout{h}", [128, TT], BF16) for h in range(4)]
    mixb = P.tile("mixb", [128, 4, TT], BF16)
    xc = [P.tile(f"xc{i}", [128, TT], F32) for i in range(3)]
    lamt = P.tile("lamt", [128, 256], F32)
    lprod = P.tile("lprod", [128, 128], F32)
    lsc = P.tile("lsc", [128, 8], F32)
    subg = P.tile("subg", [128, 1], F32)

    P.op("sp", lambda e: e.dma_start(out=biasc[:, :], in_=C.d_biasA), w=[biasc], dma=biasc)
    P.op("sp", lambda e: e.dma_start(out=lamt[:, :], in_=C.d_alam[e_]), w=[lamt], dma=lamt)
    P.op("sp", lambda e: e.dma_start(out=subg[:, :], in_=C.d_asub[e_]), w=[subg], dma=subg)
    P.op("pool", lambda e: e.dma_start(out=mask[:, :], in_=C.d_mask01), w=[mask], dma=mask)
    for h in range(4):
        for m in range(2):
            k_ = kt[h][m]
            r0_ = (4 + h) * 128 + 64 * m
            P.op("pool", lambda e, k_=k_: e.memset(k_[64:66, :], 1.0), w=[k_])
            P.op("sp", lambda e, k_=k_, r0_=r0_: e.dma_start(out=k_[0:64, :], in_=C.d_QK[r0_:r0_ + 64, :]), w=[k_], dma=k_)
            for s in range(2):
                q_ = qt[h][m][s]
                P.op("pool", lambda e, q_=q_, h=h: e.dma_start(out=q_[64:66, :], in_=C.d_qrowsA[h]), w=[q_], dma=q_)
    P.op("sp", lambda e: e.dma_start(out=vt[:, :, :], in_=C.d_V.rearrange("(c p) f -> p c f", p=128)), w=[vt], dma=vt)
    P.op("pool", lambda e: e.dma_start(out=wout[:, :, :], in_=C.d_abwout[e_]), w=[wout], dma=wout)
    P.op("dve", lambda e: e.tensor_tensor(out=lprod[:, 0:64], in0=lamt[:, 0:64], in1=lamt[:, 64:128], op=ALU.mult), r=[lamt], w=[lprod])
    P.op("dve", lambda e: e.tensor_tensor(out=lprod[:, 64:128], in0=lamt[:, 128:192], in1=lamt[:, 192:256], op=ALU.mult), r=[lamt, lprod], w=[lprod])
    P.op("act", lambda e: e.activation(out=lamt[:, 0:64], in_=lprod[:, 0:64], func=AF.Identity, accum_out=lsc[:, 0:1]), r=[lprod], w=[lamt, lsc])
    P.op("act", lambda e: e.activation(out=lamt[:, 64:128], in_=lprod[:, 64:128], func=AF.Identity, accum_out=lsc[:, 1:2]), r=[lprod, lsc], w=[lamt, lsc])
    P.op("act", lambda e: e.activation(out=lsc[:, 2:4], in_=lsc[:, 0:2], func=AF.Exp), r=[lsc], w=[lsc])
    P.op("dve", lambda e: e.tensor_tensor(out=lsc[:, 4:5], in0=lsc[:, 3:4], in1=lsc[:, 2:3], op=ALU.subtract), r=[lsc], w=[lsc])
    P.op("dve", lambda e: e.tensor_scalar(out=lsc[:, 5:6], in0=lsc[:, 4:5], scalar1=-lam_init, scalar2=None, op0=ALU.add), r=[lsc], w=[lsc])
    P.op("dve", lambda e: e.tensor_scalar(out=subg[:, :], in0=subg[:, :], scalar1=(1.0 - lam_init), scalar2=None, op0=ALU.mult), r=[subg], w=[subg])
    neglam = lsc[:, 5:6]

    pti = 0
    xci = 0
    for i in range(NT):
        t0 = i * TT
        s_ = i % 2
        P.op("sp", lambda e, t0=t0: e.dma_start(out=mixb[:, :, :], in_=C.d_MIXB.rearrange("(g p) s -> p g s", p=128)[:, :, t0:t0 + TT]),
             w=[mixb], dma=mixb)
        for h in range(4):
            for m in range(2):
                q_ = qt[h][m][s_]
                r0_ = h * 128 + 64 * m
                P.op("sp", lambda e, q_=q_, r0_=r0_, t0=t0: e.dma_start(out=q_[0:64, :], in_=C.d_QK[r0_:r0_ + 64, t0:t0 + TT]),
                     w=[q_], dma=q_)
        for h in range(4):
            for m in range(2):
                q_ = qt[h][m][s_]
                k_ = kt[h][m]
                Ob, Lb = ps[0 + 2 * m], ps[1 + 2 * m]
                nkc = 4 * i + 4
                for kc in range(nkc):
                    jd = kc - 4 * i
                    cs = 128 * jd if jd > 0 else 0
                    dl = 4 * i - kc
                    sb = ps[4 + kc % 4]
                    p_ = pt[pti % 4]
                    pti += 1
                    bcol = biasc[:, h * NDEL + dl + 3:h * NDEL + dl + 4]
                    P.op("pe", lambda e, sb=sb, k_=k_, q_=q_, kc=kc, cs=cs: e.matmul(
                        sb[:, cs:TT], lhsT=k_[0:66, kc * 128:(kc + 1) * 128], rhs=q_[0:66, cs:TT], start=True, stop=True),
                        r=[k_, q_], w=[sb])
                    P.op("act", lambda e, sb=sb, p_=p_, cs=cs, bcol=bcol: e.activation(
                        out=p_[:, cs:TT], in_=sb[:, cs:TT], func=AF.Exp, bias=bcol), r=[sb, biasc], w=[p_])
                    if jd >= 0:
                        P.op("pool", lambda e, p_=p_, jd=jd: e.tensor_tensor(
                            out=p_[:, jd * 128:(jd + 1) * 128], in0=p_[:, jd * 128:(jd + 1) * 128], in1=mask[:, :], op=ALU.mult),
                            r=[p_, mask], w=[p_])

                    def mmo(e, p_=p_, kc=kc, cs=cs, h=h, Ob=Ob, Lb=Lb, nkc=nkc):
                        e.matmul(Ob[:, cs:TT], lhsT=vt[:, kc, h * 128:(h + 1) * 128], rhs=p_[:, cs:TT],
                                 start=(kc == 0), stop=(kc == nkc - 1))
                        return e.matmul(Lb[:, cs:TT], lhsT=ones_bf[:, :], rhs=p_[:, cs:TT],
                                        start=(kc == 0), stop=(kc == nkc - 1))
                    P.op("pe", mmo, r=[vt, p_, ones_bf], w=[Ob, Lb])
                P.op("dve", lambda e, Lb=Lb: e.reciprocal(out=rl[:, :], in_=Lb[:, :]), r=[Lb], w=[rl])
                if m == 0:
                    P.op("dve", lambda e, Ob=Ob: e.tensor_tensor(out=r0[:, :], in0=Ob[:, :], in1=rl[:, :], op=ALU.mult),
                         r=[Ob, rl], w=[r0])
                else:
                    P.op("dve", lambda e, Ob=Ob: e.tensor_tensor(out=comb[:, :], in0=Ob[:, :], in1=rl[:, :], op=ALU.mult),
                         r=[Ob, rl], w=[comb])
                    P.op("dve", lambda e: e.scalar_tensor_tensor(out=comb[:, :], in0=comb[:, :], scalar=neglam, in1=r0[:, :],
                                                                 op0=ALU.mult, op1=ALU.add), r=[comb, r0, lsc], w=[comb])
            P.op("pool", lambda e: e.tensor_tensor(out=csq[:, :], in0=comb[:, :], in1=comb[:, :], op=ALU.mult), r=[comb], w=[csq])
            P.op("pe", lambda e: e.matmul(ps[1][:, :], lhsT=ones_bf[:, :], rhs=csq[:, :], start=True, stop=True),
                 r=[ones_bf, csq], w=[ps[1]])
            P.op("act", lambda e: e.activation(out=cr[:, :], in_=ps[1][:, :], func=AF.Ln, scale=1.0 / 128, bias=C.eps_col[:, 0:1]),
                 r=[ps[1], C.eps_col], w=[cr])
            P.op("act", lambda e: e.activation(out=cr[:, :], in_=cr[:, :], func=AF.Exp, scale=-0.5), r=[cr], w=[cr])
            ao = aout[h]
            P.op("dve", lambda e, ao=ao: e.scalar_tensor_tensor(out=ao[:, :], in0=comb[:, :], scalar=subg[:, 0:1], in1=cr[:, :],
                                                                op0=ALU.mult, op1=ALU.mult), r=[comb, cr, subg], w=[ao])
        rhs_chunks = [aout[j][:, :] for j in range(4)] + [mixb[:, g, :] for g in range(4)]
        C._oproj_reads = aout + [mixb]
        xci = out_proj_tile(P, C, wout, rhs_chunks, Xin, Xout, t0, xc, xci, [ps[0], ps[2]])
    P.barrier()


def odd_proj_phase(P, C, l, Xin):
    o_ = l // 2
    P.arena_reset()
    ps = C.ps
    ones_bf, gn, xa, xb, sqs, rstd = load_x_and_norm_setup(P, C, C.d_cnorm[o_])
    bd = make_bd(P)
    win = [P.tile(f"win{k}", [128, 8, 256], BF16) for k in range(5)]
    gq = P.tile("gq", [128, 2], F32)
    qf = [P.tile(f"qf{i}", [128, TT], F32) for i in range(2)]
    qsq = [P.tile(f"qsq{i}", [128, TT], BF16) for i in range(2)]
    qr = [P.tile(f"qr{i}", [128, TT], F32) for i in range(2)]
    qn = [P.tile(f"qn{i}", [128, TT], BF16) for i in range(3)]
    vtm = [P.tile(f"vtm{i}", [128, 128], BF16) for i in range(2)]
    P.op("sp", lambda e: e.dma_start(out=gq[:, :], in_=C.d_cqk[o_]), w=[gq], dma=gq)
    P.op("sp", lambda e: e.dma_start(out=xa[:, :, :], in_=Xin[:, :, 0:TT]), w=[xa], dma=xa)
    for k in range(5):
        P.op("pool", lambda e, k=k: e.dma_start(out=win[k][:, :, :], in_=C.d_cwin[o_, k]), w=[win[k]], dma=win[k])
    P.op("dve", lambda e: e.tensor_scalar(out=gq[:, 0:1], in0=gq[:, 0:1], scalar1=0.125, scalar2=None, op0=ALU.mult),
         r=[gq], w=[gq])
    rms_prologue(P, C, xa, gn, xb, sqs, rstd, ps[0], ones_bf)
    qi = 0
    for i in range(NT):
        t0 = i * TT
        if i + 1 < NT:
            P.op("sp", lambda e, t1=t0 + TT: e.dma_start(out=xa[:, :, :], in_=Xin[:, :, t1:t1 + TT]), w=[xa], dma=xa)
        for j in range(9):
            bank = ps[1 + j % 2]
            piece, sub = divmod(j, 2)

            def mm(e, bank=bank, piece=piece, sub=sub):
                ins = None
                for c in range(8):
                    ins = e.matmul(bank[:, :], lhsT=win[piece][:, c, sub * 128:(sub + 1) * 128], rhs=xb[c][:, :],
                                   start=(c == 0), stop=(c == 7))
                return ins
            P.op("pe", mm, r=[win[piece]] + xb, w=[bank])
            s = j % 2
            qo = qn[qi % 3]
            qi += 1
            gcol = gq[:, 0:1] if j < 8 else gq[:, 1:2]
            qk_norm_chunk(P, C, bank, ps[3], bd, gcol, qf[s], qsq[s], qr[s], qo, 64)
            if j < 8:
                P.op("sp", lambda e, qo=qo, j=j, t0=t0: e.dma_start(out=C.d_QK[j * 128:(j + 1) * 128, t0:t0 + TT], in_=qo[:, :]),
                     r=[qo], dma=qo)
            else:
                P.op("sp", lambda e, qo=qo, t0=t0: e.dma_start(out=C.d_KC[:, t0:t0 + TT], in_=qo[:, :]), r=[qo], dma=qo)
        for b in range(4):
            bank = ps[4 + b % 2]

            def mmv(e, bank=bank, b=b):
                ins = None
                for c in range(8):
                    ins = e.matmul(bank[:, 0:128], lhsT=xb[c][:, b * 128:(b + 1) * 128], rhs=win[4][:, c, 128:256],
                                   start=(c == 0), stop=(c == 7))
                return ins
            P.op("pe", mmv, r=[win[4]] + xb, w=[bank])
            vt = vtm[b % 2]
            P.op("act", lambda e, vt=vt, bank=bank: e.activation(out=vt[:, :], in_=bank[:, 0:128], func=AF.Identity), r=[bank], w=[vt])
            P.op("sp", lambda e, vt=vt, b=b, t0=t0: e.dma_start(out=C.d_VC[t0 + b * 128:t0 + (b + 1) * 128, :], in_=vt[:, :]),
                 r=[vt], dma=vt)
        if i + 1 < NT:
            rms_prologue(P, C, xa, gn, xb, sqs, rstd, ps[0], ones_bf)
    P.barrier()


def odd_attn_phase(P, C, l, Xin, Xout):
    o_ = l // 2
    P.arena_reset()
    ps = C.ps
    kct = [P.tile(f"kct{g}", [64, S], BF16) for g in range(2)]
    vE = [P.tile(f"vE{g}", [128, 32, 128], BF16) for g in range(2)]
    vO = [P.tile(f"vO{g}", [128, 32, 128], BF16) for g in range(2)]
    onesE = P.tile("onesE", [128, 128], BF16)
    onesO = P.tile("onesO", [128, 128], BF16)
    wout = P.tile("wout", [128, 8, 1024], BF16)
    BT = P.tile("BT", [128, 2, 2, 2, 512], F32)
    esT = P.tile("esT", [128, 2, 512], F32)
    qs = [[P.tile(f"qs{g}{s}", [64, 2, 4, TT], BF16) for s in range(2)] for g in range(2)]
    tmp = [P.tile(f"tmp{i}", [128, 512], F32) for i in range(2)]
    pt = [P.tile(f"pt{i}", [128, 512], BF16) for i in range(4)]
    lsb = P.tile("lsb", [128, 512], F32)
    rl = P.tile("rl", [128, 512], F32)
    aoutT = [P.tile(f"aoT{g}", [128, 4, TT], BF16) for g in range(2)]
    xc = [P.tile(f"xc{i}", [128, TT], F32) for i in range(3)]

    P.op("pool", lambda e: e.memset(onesE[:, :], 0.0), w=[onesE])
    P.op("pool", lambda e: e.memset(onesE[:, 0:64], 1.0), w=[onesE])
    P.op("pool", lambda e: e.memset(onesO[:, :], 0.0), w=[onesO])
    P.op("pool", lambda e: e.memset(onesO[:, 64:128], 1.0), w=[onesO])
    P.op("sp", lambda e: e.dma_start(out=BT[:, :, :, :, :], in_=C.d_BT), w=[BT], dma=BT)
    P.op("sp", lambda e: e.dma_start(out=esT[:, :, :], in_=C.d_sinks[o_]), w=[esT], dma=esT)
    P.op("act", lambda e: e.activation(out=esT[:, :, :], in_=esT[:, :, :], func=AF.Exp), r=[esT], w=[esT])
    vcv = C.d_VC.rearrange("(c p) f -> p c f", p=128)
    for g in range(2):
        P.op("sp", lambda e, g=g: e.dma_start(out=kct[g][:, :], in_=C.d_KC[g * 64:(g + 1) * 64, :]), w=[kct[g]], dma=kct[g])
        P.op("pool", lambda e, g=g: e.memset(vE[g][:, :, 64:128], 0.0), w=[vE[g]])
        P.op("pool", lambda e, g=g: e.memset(vO[g][:, :, 0:64], 0.0), w=[vO[g]])
        P.op("sp", lambda e, g=g: e.dma_start(out=vE[g][:, :, 0:64], in_=vcv[:, :, g * 64:(g + 1) * 64]), w=[vE[g]], dma=vE[g])
        P.op("sp", lambda e, g=g: e.dma_start(out=vO[g][:, :, 64:128], in_=vcv[:, :, g * 64:(g + 1) * 64]), w=[vO[g]], dma=vO[g])
    P.op("pool", lambda e: e.dma_start(out=wout[:, :, :], in_=C.d_cwout[o_]), w=[wout], dma=wout)

    qkv = C.d_QK.rearrange("(g jj par d) s -> g d par jj s", g=2, jj=4, par=2, d=64)
    pti = 0
    xci = 0
    ti = 0
    for i in range(NT):
        t0 = i * TT
        s_ = i % 2
        for g in range(2):
            for par in range(2):
                P.op("sp", lambda e, g=g, par=par, t0=t0, s_=s_: e.dma_start(out=qs[g][s_][:, par, :, :], in_=qkv[g, :, par, :, t0:t0 + TT]),
                     w=[qs[g][s_]], dma=qs[g][s_])
        for b in range(4):
            gb = 4 * i + b
            for g in range(2):
                Ob, Lb = ps[0 + 2 * g], ps[1 + 2 * g]
                chunks = [1] if gb == 0 else [0, 1]
                combos = [(ch, par) for ch in chunks for par in range(2)]
                for idx, (ch, par) in enumerate(combos):
                    kb = gb - 1 + ch
                    sb = ps[4 + idx % 4]
                    p_ = pt[pti % 4]
                    pti += 1
                    tm = tmp[ti % 2]
                    ti += 1
                    q_ = qs[g][s_]
                    P.op("pe", lambda e, sb=sb, g=g, kb=kb, q_=q_, par=par, b=b: e.matmul(
                        sb[:, :], lhsT=kct[g][:, kb * 128:(kb + 1) * 128], rhs=q_[:, par, :, b * 128:(b + 1) * 128],
                        start=True, stop=True), r=[kct[g], q_], w=[sb])
                    P.op("dve", lambda e, sb=sb, tm=tm, ch=ch, g=g, par=par: e.tensor_tensor(
                        out=tm[:, :], in0=sb[:, :], in1=BT[:, ch, g, par, :], op=ALU.add), r=[sb, BT], w=[tm])
                    P.op("act", lambda e, tm=tm, p_=p_: e.activation(out=p_[:, :], in_=tm[:, :], func=AF.Exp), r=[tm], w=[p_])
                    first = (idx == 0)
                    last = (idx == len(combos) - 1)

                    def mmo(e, p_=p_, g=g, kb=kb, par=par, Ob=Ob, Lb=Lb, first=first, last=last):
                        vv = vE[g] if par == 0 else vO[g]
                        oo = onesE if par == 0 else onesO
                        e.matmul(Ob[:, :], lhsT=vv[:, kb, :], rhs=p_[:, :], start=first, stop=last)
                        return e.matmul(Lb[:, :], lhsT=oo[:, :], rhs=p_[:, :], start=first, stop=last)
                    P.op("pe", mmo, r=[vE[g], vO[g], onesE, onesO, p_], w=[Ob, Lb])
                P.op("dve", lambda e, Lb=Lb, g=g: e.tensor_tensor(out=lsb[:, :], in0=Lb[:, :], in1=esT[:, g, :], op=ALU.add),
                     r=[Lb, esT], w=[lsb])
                P.op("dve", lambda e: e.reciprocal(out=rl[:, :], in_=lsb[:, :]), r=[lsb], w=[rl])
                ao = aoutT[g]
                P.op("dve", lambda e, Ob=Ob, ao=ao, b=b: e.tensor_tensor(
                    out=ao[:, :, b * 128:(b + 1) * 128], in0=Ob[:, :].rearrange("p (j q) -> p j q", j=4),
                    in1=rl[:, :].rearrange("p (j q) -> p j q", j=4), op=ALU.mult), r=[Ob, rl], w=[ao])
        rhs_chunks = [aoutT[j // 4][:, j % 4, :] for j in range(8)]
        C._oproj_reads = aoutT
        xci = out_proj_tile(P, C, wout, rhs_chunks, Xin, Xout, t0, xc, xci, [ps[0], ps[2]])
    P.barrier()


def build_program(phases):
    nc = bass.Bass("TRN2", target_bir_lowering=False)
    C = Ctx()
    xT = nc.dram_tensor("xT", [D, S], F32, kind="ExternalInput").ap()
    out = nc.dram_tensor("out", [D, S], F32, kind="ExternalOutput").ap()
    XA = nc.dram_tensor("XA", [D, S], F32, kind="Internal").ap()
    XB = nc.dram_tensor("XB", [D, S], F32, kind="Internal").ap()
    C.d_wup = nc.dram_tensor("f_wup", [4, NFP, 128, 8, 256], F32, kind="ExternalInput").ap()
    C.d_wdn = nc.dram_tensor("f_wdn", [4, NFP // 2, 128, 2, 1024], F32, kind="ExternalInput").ap()
    C.d_fcpar = nc.dram_tensor("f_cpar", [4, 128, 44 * 4], F32, kind="ExternalInput").ap()
    C.d_fnorm = nc.dram_tensor("f_norm", [4, 128, 8], F32, kind="ExternalInput").ap()

    def din(name, shape):
        return nc.dram_tensor(name, list(shape), F32, kind="ExternalInput").ap()
    C.d_abnorm = din("ab_norm", [2, 128, 8])
    C.d_aqk = din("a_qk", [2, 128, 2])
    C.d_bscale = din("b_scale", [2, 128, 4])
    C.d_invc = din("invc", [128, 4, TT])
    C.d_abwin = din("ab_win", [2, 4, 128, 8, 512])
    C.d_wgrp = din("b_wgrp", [2, 128, 4, 128])
    C.d_biasA = din("biasA", [128, 4 * NDEL])
    C.d_alam = din("a_lam", [2, 128, 256])
    C.d_asub = din("a_sub", [2, 128, 1])
    C.d_mask01 = din("mask01", [128, 128])
    C.d_qrowsA = din("qrowsA", [4, 2, TT])
    C.d_abwout = din("ab_wout", [2, 128, 8, 1024])
    C.d_cnorm = din("c_norm", [2, 128, 8])
    C.d_cqk = din("c_qk", [2, 128, 2])
    C.d_cwin = din("c_win", [2, 5, 128, 8, 256])
    C.d_BT = din("BT", [128, 2, 2, 2, 512])
    C.d_sinks = din("c_sinks", [2, 128, 2, 512])
    C.d_cwout = din("c_wout", [2, 128, 8, 1024])
    C.d_QK = nc.dram_tensor("sQK", [1024, S], BF16, kind="Internal").ap()
    C.d_V = nc.dram_tensor("sV", [S, 512], BF16, kind="Internal").ap()
    C.d_MIXB = nc.dram_tensor("sMIXB", [512, S], BF16, kind="Internal").ap()
    C.d_KC = nc.dram_tensor("sKC", [128, S], BF16, kind="Internal").ap()
    C.d_VC = nc.dram_tensor("sVC", [S, 128], BF16, kind="Internal").ap()

    P = Plan(nc)
    pst = nc.alloc_psum_tensor("ps", [128, 8 * 512], F32)
    C.ps = []
    for b in range(8):
        tl = Tile(pst[:, b * 512:(b + 1) * 512], Buf(f"ps{b}"))
        C.ps.append(tl)

    Plan.SB_TOP = 229344 - 64
    epst = nc.alloc_sbuf_tensor_at("eps_col", [128, 1], F32, offset=Plan.SB_TOP)
    C.eps_col = Tile(epst, Buf("eps_col"))
    P.op("pool", lambda e: e.memset(C.eps_col[:, :], EPS), w=[C.eps_col])

    def fm(ap):
        return ap.rearrange("(c p) s -> p c s", p=128)

    n = len(phases)
    cur = xT
    for pi, (kind, l) in enumerate(phases):
        if pi == n - 1:
            dst = out
        else:
            dst = XA if (pi % 2 == 0) else XB
        if kind == "ffn":
            ffn_phase(P, C, l, fm(cur), fm(dst))
        elif kind == "even":
            even_proj_phase(P, C, l, fm(cur))
            even_attn_phase(P, C, l, fm(cur), fm(dst))
        elif kind == "odd":
            odd_proj_phase(P, C, l, fm(cur))
            odd_attn_phase(P, C, l, fm(cur), fm(dst))
        else:
            raise ValueError(kind)
        cur = dst
    P.emit()
    return nc, P


def prep_weights(inp):
    W = {}
    wu = np.asarray(inp["f_w_up"], dtype=np.float32)
    wu = wu.reshape(4, 8, 128, 2, NFP, 128)
    W["f_wup"] = np.ascontiguousarray(wu.transpose(0, 4, 2, 1, 3, 5)).reshape(4, NFP, 128, 8, 256)
    wd = np.asarray(inp["f_w_down"], dtype=np.float32)
    wd = wd.reshape(4, NFP // 2, 2, 128, 1024)
    W["f_wdn"] = np.ascontiguousarray(wd.transpose(0, 1, 3, 2, 4))
    fc = np.asarray(inp["f_conv"], dtype=np.float32)
    fb = np.asarray(inp["f_conv_b"], dtype=np.float32)
    cp = np.concatenate([fc, fb[:, None, :]], axis=1)
    cp = cp.reshape(4, 4, 44, 128).transpose(0, 3, 2, 1)
    W["f_cpar"] = np.ascontiguousarray(cp).reshape(4, 128, 44 * 4)
    fn = np.asarray(inp["f_norm"], dtype=np.float32).reshape(4, 8, 128).transpose(0, 2, 1)
    W["f_norm"] = np.ascontiguousarray(fn)
    f32 = np.float32

    def A(k):
        return np.asarray(inp[k], dtype=f32)

    def pc(w):
        n = w.shape[-1]
        return w.reshape(8, 128, n).transpose(1, 0, 2)
    W["ab_norm"] = np.ascontiguousarray(A("ab_norm").reshape(2, 8, 128).transpose(0, 2, 1))
    W["c_norm"] = np.ascontiguousarray(A("c_norm").reshape(2, 8, 128).transpose(0, 2, 1))
    W["a_qk"] = np.ascontiguousarray(np.stack([np.tile(A("a_q_norm"), (1, 2)), np.tile(A("a_k_norm"), (1, 2))], axis=-1))
    W["c_qk"] = np.ascontiguousarray(np.stack([np.tile(A("c_q_norm"), (1, 2)), np.tile(A("c_k_norm"), (1, 2))], axis=-1))
    W["b_scale"] = np.ascontiguousarray(A("b_scale").reshape(2, 4, 128).transpose(0, 2, 1))
    abw = A("ab_w_in")
    W["ab_win"] = np.ascontiguousarray(np.stack([np.stack([pc(abw[e][:, 512 * k:512 * (k + 1)]) for k in range(4)]) for e in range(2)]))
    W["b_wgrp"] = np.ascontiguousarray(A("b_w_group").transpose(0, 2, 1, 3))
    W["a_lam"] = np.ascontiguousarray(np.broadcast_to(A("a_lambda").reshape(2, 1, 256), (2, 128, 256)))
    W["a_sub"] = np.ascontiguousarray(A("a_sub_norm").reshape(2, 128, 1))
    W["ab_wout"] = np.ascontiguousarray(np.stack([pc(A("ab_w_out")[e]) for e in range(2)]))
    cw = A("c_w_in")
    W["c_win"] = np.ascontiguousarray(np.stack([np.stack([pc(cw[o][:, 256 * k:256 * (k + 1)]) for k in range(5)]) for o in range(2)]))
    W["c_wout"] = np.ascontiguousarray(np.stack([pc(A("c_w_out")[o]) for o in range(2)]))
    sk = A("c_sinks")
    sk = sk.reshape(2, 2, 4, 2).transpose(0, 3, 1, 2)
    sk = np.repeat(sk[:, :, None, :, :, None], 64, axis=2)
    sk = np.broadcast_to(sk, (2, 2, 64, 2, 4, 128)).reshape(2, 128, 2, 512)
    W["c_sinks"] = np.ascontiguousarray(sk)
    W.update(const_tables())
    return W


_CONST = {}


def const_tables():
    if _CONST:
        return _CONST
    f32 = np.float32
    kr = np.arange(128, dtype=np.float64)
    bA = np.zeros((128, 4, NDEL), dtype=np.float64)
    for h in range(4):
        for di in range(NDEL):
            bA[:, h, di] = A_SLOPES[h] * (kr - 128.0 * (di - 3))
    _CONST["biasA"] = bA.reshape(128, 4 * NDEL).astype(f32)
    qc = np.arange(TT)
    qr = np.zeros((4, 2, TT), dtype=np.float64)
    for h in range(4):
        qr[h, 0] = -A_SLOPES[h] * 128.0 * (qc // 128)
        qr[h, 1] = -A_SLOPES[h] * (qc % 128)
    _CONST["qrowsA"] = qr.astype(f32)
    _CONST["mask01"] = (np.arange(128)[None, :] >= np.arange(128)[:, None]).astype(f32)
    t = np.arange(TT)
    ic = np.zeros((128, 4, TT), dtype=f32)
    for g in range(4):
        ic[:, g, :] = (1.0 / np.minimum(t + 1, 2 ** (g + 1))).astype(f32)[None, :]
    _CONST["invc"] = ic
    BT = np.zeros((128, 2, 2, 2, 4, 128), dtype=np.float64)
    krr = np.arange(128)[:, None]
    qq = np.arange(128)[None, :]
    for g in range(2):
        for par in range(2):
            for jj in range(4):
                sl = C_SLOPES[8 * g + 2 * jj + par]
                rel_prev = 128 + qq - krr
                BT[:, 0, g, par, jj, :] = np.where(rel_prev < 128, -sl * rel_prev, -30000.0)
                rel_cur = qq - krr
                BT[:, 1, g, par, jj, :] = np.where(rel_cur >= 0, -sl * rel_cur, -30000.0)
    _CONST["BT"] = BT.reshape(128, 2, 2, 2, 512).astype(f32)
    return _CONST


ALL_PHASES = [("even", 0), ("ffn", 0), ("odd", 1), ("ffn", 1), ("even", 2), ("ffn", 2), ("odd", 3), ("ffn", 3)]

_CACHE = {}


def kernel(**inputs):
    x = np.asarray(inputs["x"], dtype=np.float32)
    B = x.shape[0]
    W = prep_weights(inputs)
    key = "full"
    if key not in _CACHE:
        _CACHE[key] = build_program(ALL_PHASES)[0]
    nc = _CACHE[key]
    in_maps = []
    for b in range(B):
        m = {"xT": np.ascontiguousarray(x[b].T)}
        m.update(W)
        in_maps.append(m)
    res = run_bass_kernel_spmd(nc, in_maps, core_ids=list(range(B)))
    outs = [np.asarray(r["out"]).T for r in res.results]
    return np.ascontiguousarray(np.stack(outs, axis=0)).astype(np.float32)
```

```python
import numpy as np
import math
from contextlib import ExitStack
import concourse.bass as bass
import concourse.mybir as mybir
from concourse.bass_utils import run_bass_kernel_spmd

F32 = mybir.dt.float32
BF16 = mybir.dt.bfloat16
AF = mybir.ActivationFunctionType
ALU = mybir.AluOpType

S = 4096
D = 1024
DFF = 2816
TT = 512
NT = S // TT
EPS = 1e-6
NFP = DFF // 128


class Buf:
    __slots__ = ("name", "w", "r", "dsem")

    def __init__(self, name):
        self.name = name
        self.w = None
        self.r = {}
        self.dsem = None


class Tile:
    def __init__(self, t, buf):
        self.t = t
        self.b = buf

    def __getitem__(self, idx):
        return self.t[idx]


class Plan:
    ENGS = ("sp", "pe", "act", "dve", "pool")

    def __init__(self, nc):
        self.nc = nc
        self.ops = {e: [] for e in self.ENGS}
        self.cnt = {}
        self.waited = {e: {} for e in self.ENGS}
        self.semkeys = []
        for e in self.ENGS:
            self._newsem("e_" + e)
        self.dsem_pool = []
        self.dsem_next = 0
        self.n_ops = 0
        self.sb_off = 0
        self.sb_hi = 0
        self.uid = 0

    def _newsem(self, key):
        self.semkeys.append(key)
        self.cnt[key] = 0

    SB_BASE = 16512
    SB_TOP = 229344

    def arena_reset(self):
        self.sb_off = self.SB_BASE

    def tile(self, name, shape, dtype, nbuf=1):
        esz = 2 if dtype == BF16 else 4
        n = 1
        for s_ in shape[1:]:
            n *= s_
        nbytes = (n * esz + 63) // 64 * 64
        self.uid += 1
        t = self.nc.alloc_sbuf_tensor_at(f"{name}_{self.uid}", list(shape), dtype, offset=self.sb_off)
        self.sb_off += nbytes
        self.sb_hi = max(self.sb_hi, self.sb_off)
        assert self.sb_off <= self.SB_TOP, f"SBUF overflow at {name}: {self.sb_off}"
        return Tile(t, Buf(name))

    def op(self, eng, fn, r=(), w=(), dma=None):
        rb = [x.b if isinstance(x, Tile) else x for x in r]
        wb = [x.b if isinstance(x, Tile) else x for x in w]
        need = {}

        def add(sv):
            if sv is None:
                return
            k, v = sv
            if need.get(k, 0) < v:
                need[k] = v

        for b in rb:
            add(b.w)
        for b in wb:
            add(b.w)
            for k, v in b.r.items():
                add((k, v))
        own = "e_" + eng
        if eng == "pe":
            need.pop(own, None)
        waits = []
        wd = self.waited[eng]
        for k, v in need.items():
            if wd.get(k, 0) < v:
                wd[k] = v
                waits.append((k, v))
        if dma is not None:
            db = dma.b if isinstance(dma, Tile) else dma
            if db.dsem is None:
                if self.dsem_next < len(self.dsem_pool):
                    db.dsem = self.dsem_pool[self.dsem_next]
                else:
                    db.dsem = "d_%d" % len(self.dsem_pool)
                    self.dsem_pool.append(db.dsem)
                    self._newsem(db.dsem)
                self.dsem_next += 1
            key, inc = db.dsem, 16
        else:
            key, inc = own, 1
        self.cnt[key] += inc
        val = self.cnt[key]
        for b in wb:
            b.w = (key, val)
            b.r = {}
        for b in rb:
            if b.r.get(key, 0) < val:
                b.r[key] = val
        self.ops[eng].append((waits, fn, key, inc))
        self.n_ops += 1

    def barrier(self):
        for e in self.ENGS:
            waits = []
            wd = self.waited[e]
            for k in self.semkeys:
                v = self.cnt[k]
                if v > 0 and wd.get(k, 0) < v:
                    wd[k] = v
                    waits.append((k, v))
            if waits:
                self.ops[e].append((waits, None, None, 0))
        self.dsem_next = 0

    def emit(self):
        nc = self.nc
        with ExitStack() as st:
            semh = {}
            for k in self.semkeys:
                semh[k] = st.enter_context(nc.semaphore("s" + str(len(semh))))
            block = st.enter_context(nc.Block())
            decos = {"sp": block.sync, "pe": block.tensor, "act": block.scalar,
                     "dve": block.vector, "pool": block.gpsimd}
            for name in self.ENGS:
                ops = self.ops[name]

                def body(e, ops=ops):
                    for waits, fn, key, inc in ops:
                        for (k, v) in waits:
                            e.wait_ge(semh[k], v)
                        if fn is not None:
                            ins = fn(e)
                            ins.then_inc(semh[key], inc)

                decos[name](body)


class Ctx:
    pass


def rms_prologue(P, C, xa, gn, xb_chunks, sqs, rstd, ps_ssq, ones_bf):
    for c in range(8):
        sq = sqs[c % 2]
        P.op("pool", lambda e, c=c, sq=sq: e.tensor_tensor(out=sq[:, :], in0=xa[:, c, :], in1=xa[:, c, :], op=ALU.mult),
             r=[xa], w=[sq])
        P.op("pe", lambda e, c=c, sq=sq: e.matmul(ps_ssq[:, :], lhsT=ones_bf[:, :], rhs=sq[:, :], start=(c == 0), stop=(c == 7)),
             r=[sq, ones_bf], w=[ps_ssq])
    P.op("act", lambda e: e.activation(out=rstd[:, :], in_=ps_ssq[:, :], func=AF.Ln, scale=1.0 / D, bias=C.eps_col[:, 0:1]),
         r=[ps_ssq, C.eps_col], w=[rstd])
    P.op("act", lambda e: e.activation(out=rstd[:, :], in_=rstd[:, :], func=AF.Exp, scale=-0.5), r=[rstd], w=[rstd])
    for c in range(8):
        P.op("dve", lambda e, c=c: e.scalar_tensor_tensor(out=xb_chunks[c][:, :], in0=xa[:, c, :], scalar=gn[:, c:c + 1],
                                                          in1=rstd[:, :], op0=ALU.mult, op1=ALU.mult),
             r=[xa, gn, rstd], w=[xb_chunks[c]])


def ffn_phase(P, C, l, Xin, Xout):
    nc = P.nc
    P.arena_reset()
    ones_bf = P.tile("ones", [128, 128], BF16)
    wup = [P.tile(f"wup{p}", [128, 8, 256], BF16) for p in range(NFP)]
    wdn = [P.tile(f"wdn{j}", [128, 2, 1024], BF16) for j in range(NFP // 2)]
    cpar = P.tile("cpar", [128, 44 * 4], F32)
    gn = P.tile("gn", [128, 8], F32)
    xa = P.tile("xa", [128, 8, TT], F32)
    xb = [P.tile(f"xb{c}", [128, TT], BF16) for c in range(8)]
    sqs = [P.tile(f"sq{i}", [128, TT], BF16) for i in range(2)]
    rstd = P.tile("rstd", [128, TT], F32)
    cg = [P.tile(f"cg{i}", [128, TT], F32) for i in range(2)]
    cu = [P.tile(f"cu{i}", [128, TT], F32) for i in range(2)]
    sg = [P.tile(f"sg{i}", [128, TT], F32) for i in range(2)]
    act = [P.tile(f"act{f}", [128, TT], BF16) for f in range(NFP)]
    xc = [P.tile(f"xc{i}", [128, TT], F32) for i in range(4)]
    halo = P.tile("halo", [128, 44, 2], F32)
    ps = C.ps

    P.op("pool", lambda e: e.memset(ones_bf[:, :], 1.0), w=[ones_bf])
    P.op("pool", lambda e: e.memset(halo[:, :, :], 0.0), w=[halo])
    P.op("sp", lambda e: e.dma_start(out=cpar[:, :], in_=C.d_fcpar[l]), w=[cpar], dma=cpar)
    P.op("sp", lambda e: e.dma_start(out=gn[:, :], in_=C.d_fnorm[l]), w=[gn], dma=gn)
    P.op("sp", lambda e: e.dma_start(out=xa[:, :, :], in_=Xin[:, :, 0:TT]), w=[xa], dma=xa)
    for p in range(NFP):
        P.op("pool", lambda e, p=p: e.dma_start(out=wup[p][:, :, :], in_=C.d_wup[l, p]), w=[wup[p]], dma=wup[p])
    for j in range(NFP // 2):
        P.op("pool", lambda e, j=j: e.dma_start(out=wdn[j][:, :, :], in_=C.d_wdn[l, j]), w=[wdn[j]], dma=wdn[j])

    rms_prologue(P, C, xa, gn, xb, sqs, rstd, ps[0], ones_bf)

    xci = 0
    for i in range(NT):
        t0 = i * TT
        if i + 1 < NT:
            P.op("sp", lambda e, t1=t0 + TT: e.dma_start(out=xa[:, :, :], in_=Xin[:, :, t1:t1 + TT]), w=[xa], dma=xa)
        for f in range(NFP):
            s = f % 2
            for gu in range(2):
                bank = ps[1 + 2 * s + gu]
                ch = f + gu * NFP
                cdst = (cg if gu == 0 else cu)[s]
                w0 = cpar[:, 4 * ch + 0:4 * ch + 1]
                w1 = cpar[:, 4 * ch + 1:4 * ch + 2]
                w2 = cpar[:, 4 * ch + 2:4 * ch + 3]
                bb = cpar[:, 4 * ch + 3:4 * ch + 4]
                hl = halo

                def mm(e, f=f, gu=gu, bank=bank):
                    ins = None
                    for c in range(8):
                        ins = e.matmul(bank[:, :], lhsT=wup[f][:, c, gu * 128:(gu + 1) * 128], rhs=xb[c][:, :],
                                       start=(c == 0), stop=(c == 7))
                    return ins
                P.op("pe", mm, r=[wup[f]] + xb, w=[bank])
                P.op("act", lambda e, bank=bank, cdst=cdst, w2=w2, bb=bb: e.activation(
                    out=cdst[:, :], in_=bank[:, :], func=AF.Identity, bias=bb, scale=w2), r=[bank, cpar], w=[cdst])
                P.op("dve", lambda e, bank=bank, cdst=cdst, w1=w1: e.scalar_tensor_tensor(
                    out=cdst[:, 1:TT], in0=bank[:, 0:TT - 1], scalar=w1, in1=cdst[:, 1:TT], op0=ALU.mult, op1=ALU.add),
                    r=[bank, cpar, cdst], w=[cdst])
                P.op("dve", lambda e, bank=bank, cdst=cdst, w0=w0: e.scalar_tensor_tensor(
                    out=cdst[:, 2:TT], in0=bank[:, 0:TT - 2], scalar=w0, in1=cdst[:, 2:TT], op0=ALU.mult, op1=ALU.add),
                    r=[bank, cpar, cdst], w=[cdst])
                P.op("dve", lambda e, ch=ch, cdst=cdst, w1=w1: e.scalar_tensor_tensor(
                    out=cdst[:, 0:1], in0=hl[:, ch, 1:2], scalar=w1, in1=cdst[:, 0:1], op0=ALU.mult, op1=ALU.add),
                    r=[hl, cpar, cdst], w=[cdst])
                P.op("dve", lambda e, ch=ch, cdst=cdst, w0=w0: e.scalar_tensor_tensor(
                    out=cdst[:, 0:2], in0=hl[:, ch, 0:2], scalar=w0, in1=cdst[:, 0:2], op0=ALU.mult, op1=ALU.add),
                    r=[hl, cpar, cdst], w=[cdst])
                P.op("dve", lambda e, ch=ch, bank=bank: e.tensor_copy(out=hl[:, ch, :], in_=bank[:, TT - 2:TT]),
                     r=[bank], w=[hl])
            P.op("act", lambda e, s=s: e.activation(out=sg[s][:, :], in_=cg[s][:, :], func=AF.Silu), r=[cg[s]], w=[sg[s]])
            P.op("pool", lambda e, s=s, f=f: e.tensor_tensor(out=act[f][:, :], in0=sg[s][:, :], in1=cu[s][:, :], op=ALU.mult),
                 r=[sg[s], cu[s]], w=[act[f]])
        if i + 1 < NT:
            rms_prologue(P, C, xa, gn, xb, sqs, rstd, ps[0], ones_bf)
        for c in range(8):
            bank = ps[5 + c % 2]
            xt = xc[xci % 4]
            xci += 1
            P.op("sp", lambda e, xt=xt, c=c, t0=t0: e.dma_start(out=xt[:, :], in_=Xin[:, c, t0:t0 + TT]), w=[xt], dma=xt)

            def mm2(e, c=c, bank=bank):
                ins = None
                for f in range(NFP):
                    ins = e.matmul(bank[:, :], lhsT=wdn[f // 2][:, f % 2, c * 128:(c + 1) * 128], rhs=act[f][:, :],
                                   start=(f == 0), stop=(f == NFP - 1))
                return ins
            P.op("pe", mm2, r=wdn + act, w=[bank])
            P.op("dve", lambda e, xt=xt, bank=bank: e.tensor_tensor(out=xt[:, :], in0=bank[:, :], in1=xt[:, :], op=ALU.add),
                 r=[bank, xt], w=[xt])
            P.op("sp", lambda e, xt=xt, c=c, t0=t0: e.dma_start(out=Xout[:, c, t0:t0 + TT], in_=xt[:, :]), r=[xt], dma=xt)
    P.barrier()


A_SLOPES = [2.0 ** (-8.0 * (h + 1) / 4) for h in range(4)]
C_SLOPES = [2.0 ** (-8.0 * (h + 1) / 16) for h in range(16)]
NDEL = 35


def load_x_and_norm_setup(P, C, d_norm_l):
    ones_bf = P.tile("ones", [128, 128], BF16)
    gn = P.tile("gn", [128, 8], F32)
    xa = P.tile("xa", [128, 8, TT], F32)
    xb = [P.tile(f"xb{c}", [128, TT], BF16) for c in range(8)]
    sqs = [P.tile(f"sq{i}", [128, TT], BF16) for i in range(2)]
    rstd = P.tile("rstd", [128, TT], F32)
    P.op("pool", lambda e: e.memset(ones_bf[:, :], 1.0), w=[ones_bf])
    P.op("sp", lambda e: e.dma_start(out=gn[:, :], in_=d_norm_l), w=[gn], dma=gn)
    return ones_bf, gn, xa, xb, sqs, rstd


def qk_norm_part1(P, C, bank, qf, qsq):
    P.op("act", lambda e: e.activation(out=qf[:, :], in_=bank[:, :], func=AF.Identity), r=[bank], w=[qf])
    P.op("pool", lambda e: e.tensor_tensor(out=qsq[:, :], in0=qf[:, :], in1=qf[:, :], op=ALU.mult), r=[qf], w=[qsq])


def qk_norm_part2(P, C, ssq_bank, bd, gcol, qf, qsq, qr, qn, nfeat):
    P.op("pe", lambda e: e.matmul(ssq_bank[:, :], lhsT=bd[:, :], rhs=qsq[:, :], start=True, stop=True),
         r=[bd, qsq], w=[ssq_bank])
    P.op("act", lambda e: e.activation(out=qr[:, :], in_=ssq_bank[:, :], func=AF.Ln, scale=1.0 / nfeat, bias=C.eps_col[:, 0:1]),
         r=[ssq_bank, C.eps_col], w=[qr])
    P.op("act", lambda e: e.activation(out=qr[:, :], in_=qr[:, :], func=AF.Exp, scale=-0.5), r=[qr], w=[qr])
    P.op("dve", lambda e: e.scalar_tensor_tensor(out=qn[:, :], in0=qf[:, :], scalar=gcol, in1=qr[:, :],
                                                 op0=ALU.mult, op1=ALU.mult), r=[qf, qr], w=[qn])


def make_bd(P):
    bd = P.tile("bd", [128, 128], BF16)
    P.op("pool", lambda e: e.memset(bd[:, :], 0.0), w=[bd])
    P.op("pool", lambda e: e.memset(bd[0:64, 0:64], 1.0), w=[bd])
    P.op("pool", lambda e: e.memset(bd[64:128, 64:128], 1.0), w=[bd])
    return bd


def even_proj_phase(P, C, l, Xin):
    e_ = l // 2
    P.arena_reset()
    ps = C.ps
    ones_bf, gn, xa, xb, sqs, rstd = load_x_and_norm_setup(P, C, C.d_abnorm[e_])
    bd = make_bd(P)
    win = [P.tile(f"win{k}", [128, 8, 512], BF16) for k in range(4)]
    wgrp = P.tile("wgrp", [128, 4, 128], BF16)
    gq = P.tile("gq", [128, 2], F32)
    bsc = P.tile("bsc", [128, 4], F32)
    invc = P.tile("invc", [128, 4, TT], F32)
    qf = [P.tile(f"qf{i}", [128, TT], F32) for i in range(2)]
    qsq = [P.tile(f"qsq{i}", [128, TT], BF16) for i in range(2)]
    qr = [P.tile(f"qr{i}", [128, TT], F32) for i in range(2)]
    qn = [P.tile(f"qn{i}", [128, TT], BF16) for i in range(3)]
    vtm = [P.tile(f"vtm{i}", [128, 512], BF16) for i in range(2)]
    ubuf = [P.tile(f"ubuf{g}", [128, TT + 16], F32) for g in range(4)]
    Tt = [P.tile(f"Tt{i}", [128, TT + 16], F32) for i in range(2)]
    pooled = [P.tile(f"pooled{i}", [128, TT], BF16) for i in range(4)]
    bo = [P.tile(f"bo{i}", [128, TT], BF16) for i in range(2)]

    P.op("sp", lambda e: e.dma_start(out=gq[:, :], in_=C.d_aqk[e_]), w=[gq], dma=gq)
    P.op("sp", lambda e: e.dma_start(out=bsc[:, :], in_=C.d_bscale[e_]), w=[bsc], dma=bsc)
    P.op("sp", lambda e: e.dma_start(out=invc[:, :, :], in_=C.d_invc), w=[invc], dma=invc)
    P.op("sp", lambda e: e.dma_start(out=xa[:, :, :], in_=Xin[:, :, 0:TT]), w=[xa], dma=xa)
    for k in range(4):
        P.op("pool", lambda e, k=k: e.dma_start(out=win[k][:, :, :], in_=C.d_abwin[e_, k]), w=[win[k]], dma=win[k])
    P.op("pool", lambda e: e.dma_start(out=wgrp[:, :, :], in_=C.d_wgrp[e_]), w=[wgrp], dma=wgrp)
    P.op("dve", lambda e: e.tensor_scalar(out=gq[:, 0:1], in0=gq[:, 0:1], scalar1=0.125, scalar2=None, op0=ALU.mult),
         r=[gq], w=[gq])
    for g in range(4):
        P.op("pool", lambda e, g=g: e.memset(ubuf[g][:, 0:16], 0.0), w=[ubuf[g]])

    rms_prologue(P, C, xa, gn, xb, sqs, rstd, ps[0], ones_bf)
    qi = 0
    for i in range(NT):
        t0 = i * TT
        if i + 1 < NT:
            P.op("sp", lambda e, t1=t0 + TT: e.dma_start(out=xa[:, :, :], in_=Xin[:, :, t1:t1 + TT]), w=[xa], dma=xa)
        pend = None
        for j in range(8):
            bank = ps[1 + j % 2]
            piece, sub = divmod(j, 4)

            def mm(e, bank=bank, piece=piece, sub=sub):
                ins = None
                for c in range(8):
                    ins = e.matmul(bank[:, :], lhsT=win[piece][:, c, sub * 128:(sub + 1) * 128], rhs=xb[c][:, :],
                                   start=(c == 0), stop=(c == 7))
                return ins
            P.op("pe", mm, r=[win[piece]] + xb, w=[bank])
            s = j % 2
            qo = qn[qi % 3]
            qi += 1
            qk_norm_part1(P, C, bank, qf[s], qsq[s])
            if pend is not None:
                pend()

            def tail(j=j, s=s, qo=qo, piece=piece, t0=t0):
                qk_norm_part2(P, C, ps[3 if s == 0 else 7], bd, gq[:, piece:piece + 1], qf[s], qsq[s], qr[s], qo, 64)
                P.op("sp", lambda e: e.dma_start(out=C.d_QK[j * 128:(j + 1) * 128, t0:t0 + TT], in_=qo[:, :]),
                     r=[qo], dma=qo)
            pend = tail
        for b in range(4):
            bank = ps[4 + b % 2]

            def mmv(e, bank=bank, b=b):
                ins = None
                for c in range(8):
                    ins = e.matmul(bank[:, :], lhsT=xb[c][:, b * 128:(b + 1) * 128], rhs=win[2][:, c, :],
                                   start=(c == 0), stop=(c == 7))
                return ins
            P.op("pe", mmv, r=[win[2]] + xb, w=[bank])
            vt = vtm[b % 2]
            P.op("act", lambda e, vt=vt, bank=bank: e.activation(out=vt[:, :], in_=bank[:, :], func=AF.Identity), r=[bank], w=[vt])
            P.op("sp", lambda e, vt=vt, b=b, t0=t0: e.dma_start(out=C.d_V[t0 + b * 128:t0 + (b + 1) * 128, :], in_=vt[:, :]),
                 r=[vt], dma=vt)
        pend()
        mixq = []
        for g in range(4):
            bank = ps[4 + g % 2]
            wlen = 2 ** (g + 1)

            def mmu(e, bank=bank, g=g):
                ins = None
                for c in range(8):
                    ins = e.matmul(bank[:, :], lhsT=win[3][:, c, g * 128:(g + 1) * 128], rhs=xb[c][:, :],
                                   start=(c == 0), stop=(c == 7))
                return ins
            P.op("pe", mmu, r=[win[3]] + xb, w=[bank])
            ub = ubuf[g]
            P.op("act", lambda e, ub=ub, bank=bank: e.activation(out=ub[:, 16:16 + TT], in_=bank[:, :], func=AF.Identity),
                 r=[bank], w=[ub])
            src = ub
            for k in range(g + 1):
                sh = 2 ** k
                lo = 2 ** (k + 1) - 1
                dst = Tt[k % 2]
                P.op("pool", lambda e, src=src, dst=dst, lo=lo, sh=sh: e.tensor_tensor(
                    out=dst[:, lo:TT + 16], in0=src[:, lo:TT + 16], in1=src[:, lo - sh:TT + 16 - sh], op=ALU.add),
                    r=[src], w=[dst])
                src = dst
            pl = pooled[g]
            if i == 0:
                P.op("pool", lambda e, src=src, g=g: e.tensor_tensor(out=src[:, 16:16 + TT], in0=src[:, 16:16 + TT],
                                                                     in1=invc[:, g, :], op=ALU.mult), r=[src, invc], w=[src])
                sc = 1.0
            else:
                sc = 1.0 / wlen
            P.op("dve", lambda e, src=src, ub=ub, pl=pl, sc=sc: e.scalar_tensor_tensor(
                out=pl[:, :], in0=src[:, 16:16 + TT], scalar=sc, in1=ub[:, 16:16 + TT], op0=ALU.mult, op1=ALU.subtract),
                r=[src, ub], w=[pl])
            P.op("pool", lambda e, ub=ub: e.tensor_copy(out=ub[:, 0:16], in_=ub[:, TT:TT + 16]), r=[ub], w=[ub])
            def mixtail(pl=pl, g=g, t0=t0):
                bt = bo[g % 2]
                P.op("pe", lambda e: e.matmul(ps[6][:, :], lhsT=wgrp[:, g, :], rhs=pl[:, :], start=True, stop=True),
                     r=[wgrp, pl], w=[ps[6]])
                P.op("act", lambda e: e.activation(out=bt[:, :], in_=ps[6][:, :], func=AF.Identity, scale=bsc[:, g:g + 1]),
                     r=[ps[6], bsc], w=[bt])
                P.op("sp", lambda e: e.dma_start(out=C.d_MIXB[g * 128:(g + 1) * 128, t0:t0 + TT], in_=bt[:, :]),
                     r=[bt], dma=bt)
            mixq.append(mixtail)
        for mt in mixq:
            mt()
        if i + 1 < NT:
            rms_prologue(P, C, xa, gn, xb, sqs, rstd, ps[0], ones_bf)
    P.barrier()


def out_proj_tile(P, C, wout, rhs_chunks, Xin, Xout, t0, xc, xci, banks):
    for c in range(8):
        bank = banks[c % 2]
        xt = xc[xci % len(xc)]
        xci += 1
        P.op("sp", lambda e, xt=xt, c=c: e.dma_start(out=xt[:, :], in_=Xin[:, c, t0:t0 + TT]), w=[xt], dma=xt)

        def mm(e, c=c, bank=bank):
            ins = None
            for j in range(8):
                ins = e.matmul(bank[:, :], lhsT=wout[:, j, c * 128:(c + 1) * 128], rhs=rhs_chunks[j],
                               start=(j == 0), stop=(j == 7))
            return ins
        P.op("pe", mm, r=[wout] + C._oproj_reads, w=[bank])
        P.op("dve", lambda e, xt=xt, bank=bank: e.tensor_tensor(out=xt[:, :], in0=bank[:, :], in1=xt[:, :], op=ALU.add),
             r=[bank, xt], w=[xt])
        P.op("sp", lambda e, xt=xt, c=c: e.dma_start(out=Xout[:, c, t0:t0 + TT], in_=xt[:, :]), r=[xt], dma=xt)
    return xci


def even_attn_phase(P, C, l, Xin, Xout):
    e_ = l // 2
    lam_init = 0.8 - 0.6 * math.exp(-0.3 * l)
    P.arena_reset()
    ps = C.ps
    ones_bf = P.tile("ones", [128, 128], BF16)
    P.op("pool", lambda e: e.memset(ones_bf[:, :], 1.0), w=[ones_bf])
    kt = [[P.tile(f"kt{h}{m}", [66, S], BF16) for m in range(2)] for h in range(4)]
    vt = P.tile("vt", [128, 32, 512], BF16)
    wout = P.tile("wout", [128, 8, 1024], BF16)
    biasc = P.tile("biasc", [128, 4 * NDEL], F32)
    mask = P.tile("mask", [128, 128], BF16)
    qt = [[[P.tile(f"qt{h}{m}{s}", [66, TT], BF16) for s in range(2)] for m in range(2)] for h in range(4)]
    pt = [P.tile(f"pt{i}", [128, TT], BF16) for i in range(4)]
    r0 = P.tile("r0", [128, TT], F32)
    rl = P.tile("rl", [128, TT], F32)
    combs = [P.tile(f"comb{i}", [128, TT], F32) for i in range(2)]
    csqs = [P.tile(f"csq{i}", [128, TT], BF16) for i in range(2)]
    cr = P.tile("cr", [128, TT], F32)
    aout = [P.tile(f"aout{h}", [128, TT], BF16) for h in range(4)]
    mixb = P.tile("mixb", [128, 4, TT], BF16)
    xc = [P.tile(f"xc{i}", [128, TT], F32) for i in range(3)]
    lamt = P.tile("lamt", [128, 256], F32)
    lprod = P.tile("lprod", [128, 128], F32)
    lsc = P.tile("lsc", [128, 8], F32)
    subg = P.tile("subg", [128, 1], F32)

    P.op("sp", lambda e: e.dma_start(out=biasc[:, :], in_=C.d_biasA), w=[biasc], dma=biasc)
    P.op("sp", lambda e: e.dma_start(out=lamt[:, :], in_=C.d_alam[e_]), w=[lamt], dma=lamt)
    P.op("sp", lambda e: e.dma_start(out=subg[:, :], in_=C.d_asub[e_]), w=[subg], dma=subg)
    P.op("pool", lambda e: e.dma_start(out=mask[:, :], in_=C.d_mask01), w=[mask], dma=mask)
    for h in range(4):
        for m in range(2):
            k_ = kt[h][m]
            r0_ = (4 + h) * 128 + 64 * m
            P.op("pool", lambda e, k_=k_: e.memset(k_[64:66, :], 1.0), w=[k_])
            P.op("sp", lambda e, k_=k_, r0_=r0_: e.dma_start(out=k_[0:64, :], in_=C.d_QK[r0_:r0_ + 64, :]), w=[k_], dma=k_)
            for s in range(2):
                q_ = qt[h][m][s]
                P.op("pool", lambda e, q_=q_, h=h: e.dma_start(out=q_[64:66, :], in_=C.d_qrowsA[h]), w=[q_], dma=q_)
    P.op("sp", lambda e: e.dma_start(out=vt[:, :, :], in_=C.d_V.rearrange("(c p) f -> p c f", p=128)), w=[vt], dma=vt)
    P.op("pool", lambda e: e.dma_start(out=wout[:, :, :], in_=C.d_abwout[e_]), w=[wout], dma=wout)
    P.op("dve", lambda e: e.tensor_tensor(out=lprod[:, 0:64], in0=lamt[:, 0:64], in1=lamt[:, 64:128], op=ALU.mult), r=[lamt], w=[lprod])
    P.op("dve", lambda e: e.tensor_tensor(out=lprod[:, 64:128], in0=lamt[:, 128:192], in1=lamt[:, 192:256], op=ALU.mult), r=[lamt, lprod], w=[lprod])
    P.op("act", lambda e: e.activation(out=lamt[:, 0:64], in_=lprod[:, 0:64], func=AF.Identity, accum_out=lsc[:, 0:1]), r=[lprod], w=[lamt, lsc])
    P.op("act", lambda e: e.activation(out=lamt[:, 64:128], in_=lprod[:, 64:128], func=AF.Identity, accum_out=lsc[:, 1:2]), r=[lprod, lsc], w=[lamt, lsc])
    P.op("act", lambda e: e.activation(out=lsc[:, 2:4], in_=lsc[:, 0:2], func=AF.Exp), r=[lsc], w=[lsc])
    P.op("dve", lambda e: e.tensor_tensor(out=lsc[:, 4:5], in0=lsc[:, 3:4], in1=lsc[:, 2:3], op=ALU.subtract), r=[lsc], w=[lsc])
    P.op("dve", lambda e: e.tensor_scalar(out=lsc[:, 5:6], in0=lsc[:, 4:5], scalar1=-lam_init, scalar2=None, op0=ALU.add), r=[lsc], w=[lsc])
    P.op("dve", lambda e: e.tensor_scalar(out=subg[:, :], in0=subg[:, :], scalar1=(1.0 - lam_init), scalar2=None, op0=ALU.mult), r=[subg], w=[subg])
    neglam = lsc[:, 5:6]

    pti = 0
    xci = 0
    LA = 2
    sbanks = [ps[4], ps[5], ps[6]]
    ssqb = ps[7]
    for i in range(NT):
        t0 = i * TT
        s_ = i % 2
        P.op("sp", lambda e, t0=t0: e.dma_start(out=mixb[:, :, :], in_=C.d_MIXB.rearrange("(g p) s -> p g s", p=128)[:, :, t0:t0 + TT]),
             w=[mixb], dma=mixb)
        for h in range(4):
            for m in range(2):
                q_ = qt[h][m][s_]
                r0_ = h * 128 + 64 * m
                P.op("sp", lambda e, q_=q_, r0_=r0_, t0=t0: e.dma_start(out=q_[0:64, :], in_=C.d_QK[r0_:r0_ + 64, t0:t0 + TT]),
                     w=[q_], dma=q_)
        nkc = 4 * i + 4
        its = [(h, m, kc) for h in range(4) for m in range(2) for kc in range(nkc)]
        slot = {}
        deferred = []

        def stageA(n):
            nonlocal pti
            h, m, kc = its[n]
            q_ = qt[h][m][s_]
            k_ = kt[h][m]
            jd = kc - 4 * i
            cs = 128 * jd if jd > 0 else 0
            dl = 4 * i - kc
            sb = sbanks[pti % 3]
            p_ = pt[pti % 4]
            pti += 1
            slot[n] = (p_, cs)
            bcol = biasc[:, h * NDEL + dl + 3:h * NDEL + dl + 4]
            P.op("pe", lambda e: e.matmul(sb[:, cs:TT], lhsT=k_[0:66, kc * 128:(kc + 1) * 128], rhs=q_[0:66, cs:TT],
                                          start=True, stop=True), r=[k_, q_], w=[sb])
            P.op("act", lambda e: e.activation(out=p_[:, cs:TT], in_=sb[:, cs:TT], func=AF.Exp, bias=bcol),
                 r=[sb, biasc], w=[p_])
            if jd >= 0:
                P.op("pool", lambda e: e.tensor_tensor(out=p_[:, jd * 128:(jd + 1) * 128], in0=p_[:, jd * 128:(jd + 1) * 128],
                                                       in1=mask[:, :], op=ALU.mult), r=[p_, mask], w=[p_])

        def fin2(h):
            cb = combs[h % 2]
            cq = csqs[h % 2]
            ao = aout[h]
            P.op("pe", lambda e: e.matmul(ssqb[:, :], lhsT=ones_bf[:, :], rhs=cq[:, :], start=True, stop=True),
                 r=[ones_bf, cq], w=[ssqb])
            P.op("act", lambda e: e.activation(out=cr[:, :], in_=ssqb[:, :], func=AF.Ln, scale=1.0 / 128, bias=C.eps_col[:, 0:1]),
                 r=[ssqb, C.eps_col], w=[cr])
            P.op("act", lambda e: e.activation(out=cr[:, :], in_=cr[:, :], func=AF.Exp, scale=-0.5), r=[cr], w=[cr])
            P.op("dve", lambda e: e.scalar_tensor_tensor(out=ao[:, :], in0=cb[:, :], scalar=subg[:, 0:1], in1=cr[:, :],
                                                         op0=ALU.mult, op1=ALU.mult), r=[cb, cr, subg], w=[ao])

        def stageB(n):
            h, m, kc = its[n]
            p_, cs = slot.pop(n)
            Ob, Lb = ps[0 + 2 * m], ps[1 + 2 * m]
            nk = len(its) // 8

            def mmo(e):
                e.matmul(Ob[:, cs:TT], lhsT=vt[:, kc, h * 128:(h + 1) * 128], rhs=p_[:, cs:TT],
                         start=(kc == 0), stop=(kc == nk - 1))
                return e.matmul(Lb[:, cs:TT], lhsT=ones_bf[:, :], rhs=p_[:, cs:TT],
                                start=(kc == 0), stop=(kc == nk - 1))
            P.op("pe", mmo, r=[vt, p_, ones_bf], w=[Ob, Lb])
            if kc == nk - 1:
                cb = combs[h % 2]
                cq = csqs[h % 2]
                P.op("dve", lambda e: e.reciprocal(out=rl[:, :], in_=Lb[:, :]), r=[Lb], w=[rl])
                if m == 0:
                    P.op("dve", lambda e: e.tensor_tensor(out=r0[:, :], in0=Ob[:, :], in1=rl[:, :], op=ALU.mult),
                         r=[Ob, rl], w=[r0])
                else:
                    P.op("dve", lambda e: e.tensor_tensor(out=cb[:, :], in0=Ob[:, :], in1=rl[:, :], op=ALU.mult),
                         r=[Ob, rl], w=[cb])
                    P.op("dve", lambda e: e.scalar_tensor_tensor(out=cb[:, :], in0=cb[:, :], scalar=neglam, in1=r0[:, :],
                                                                 op0=ALU.mult, op1=ALU.add), r=[cb, r0, lsc], w=[cb])
                    P.op("pool", lambda e: e.tensor_tensor(out=cq[:, :], in0=cb[:, :], in1=cb[:, :], op=ALU.mult), r=[cb], w=[cq])
                    deferred.append((n + 6, h))

        nit = len(its)
        for n in range(nit + LA):
            if n < nit:
                stageA(n)
            nb = n - LA
            if nb >= 0:
                stageB(nb)
                while deferred and deferred[0][0] <= nb:
                    fin2(deferred.pop(0)[1])
        while deferred:
            fin2(deferred.pop(0)[1])
        rhs_chunks = [aout[j][:, :] for j in range(4)] + [mixb[:, g, :] for g in range(4)]
        C._oproj_reads = aout + [mixb]
        xci = out_proj_tile(P, C, wout, rhs_chunks, Xin, Xout, t0, xc, xci, [ps[0], ps[2]])
    P.barrier()


def odd_proj_phase(P, C, l, Xin):
    o_ = l // 2
    P.arena_reset()
    ps = C.ps
    ones_bf, gn, xa, xb, sqs, rstd = load_x_and_norm_setup(P, C, C.d_cnorm[o_])
    bd = make_bd(P)
    win = [P.tile(f"win{k}", [128, 8, 256], BF16) for k in range(5)]
    gq = P.tile("gq", [128, 2], F32)
    qf = [P.tile(f"qf{i}", [128, TT], F32) for i in range(2)]
    qsq = [P.tile(f"qsq{i}", [128, TT], BF16) for i in range(2)]
    qr = [P.tile(f"qr{i}", [128, TT], F32) for i in range(2)]
    qn = [P.tile(f"qn{i}", [128, TT], BF16) for i in range(3)]
    vtm = [P.tile(f"vtm{i}", [128, 128], BF16) for i in range(2)]
    P.op("sp", lambda e: e.dma_start(out=gq[:, :], in_=C.d_cqk[o_]), w=[gq], dma=gq)
    P.op("sp", lambda e: e.dma_start(out=xa[:, :, :], in_=Xin[:, :, 0:TT]), w=[xa], dma=xa)
    for k in range(5):
        P.op("pool", lambda e, k=k: e.dma_start(out=win[k][:, :, :], in_=C.d_cwin[o_, k]), w=[win[k]], dma=win[k])
    P.op("dve", lambda e: e.tensor_scalar(out=gq[:, 0:1], in0=gq[:, 0:1], scalar1=0.125, scalar2=None, op0=ALU.mult),
         r=[gq], w=[gq])
    rms_prologue(P, C, xa, gn, xb, sqs, rstd, ps[0], ones_bf)
    qi = 0
    for i in range(NT):
        t0 = i * TT
        if i + 1 < NT:
            P.op("sp", lambda e, t1=t0 + TT: e.dma_start(out=xa[:, :, :], in_=Xin[:, :, t1:t1 + TT]), w=[xa], dma=xa)
        pend = None
        for j in range(9):
            bank = ps[1 + j % 2]
            piece, sub = divmod(j, 2)

            def mm(e, bank=bank, piece=piece, sub=sub):
                ins = None
                for c in range(8):
                    ins = e.matmul(bank[:, :], lhsT=win[piece][:, c, sub * 128:(sub + 1) * 128], rhs=xb[c][:, :],
                                   start=(c == 0), stop=(c == 7))
                return ins
            P.op("pe", mm, r=[win[piece]] + xb, w=[bank])
            s = j % 2
            qo = qn[qi % 3]
            qi += 1
            qk_norm_part1(P, C, bank, qf[s], qsq[s])
            if pend is not None:
                pend()

            def tail(j=j, s=s, qo=qo, t0=t0):
                gcol = gq[:, 0:1] if j < 8 else gq[:, 1:2]
                qk_norm_part2(P, C, ps[3 if s == 0 else 7], bd, gcol, qf[s], qsq[s], qr[s], qo, 64)
                if j < 8:
                    P.op("sp", lambda e: e.dma_start(out=C.d_QK[j * 128:(j + 1) * 128, t0:t0 + TT], in_=qo[:, :]),
                         r=[qo], dma=qo)
                else:
                    P.op("sp", lambda e: e.dma_start(out=C.d_KC[:, t0:t0 + TT], in_=qo[:, :]), r=[qo], dma=qo)
            pend = tail
        for b in range(4):
            bank = ps[4 + b % 2]

            def mmv(e, bank=bank, b=b):
                ins = None
                for c in range(8):
                    ins = e.matmul(bank[:, 0:128], lhsT=xb[c][:, b * 128:(b + 1) * 128], rhs=win[4][:, c, 128:256],
                                   start=(c == 0), stop=(c == 7))
                return ins
            P.op("pe", mmv, r=[win[4]] + xb, w=[bank])
            vt = vtm[b % 2]
            P.op("act", lambda e, vt=vt, bank=bank: e.activation(out=vt[:, :], in_=bank[:, 0:128], func=AF.Identity), r=[bank], w=[vt])
            P.op("sp", lambda e, vt=vt, b=b, t0=t0: e.dma_start(out=C.d_VC[t0 + b * 128:t0 + (b + 1) * 128, :], in_=vt[:, :]),
                 r=[vt], dma=vt)
        pend()
        if i + 1 < NT:
            rms_prologue(P, C, xa, gn, xb, sqs, rstd, ps[0], ones_bf)
    P.barrier()


def odd_attn_phase(P, C, l, Xin, Xout):
    o_ = l // 2
    P.arena_reset()
    ps = C.ps
    kct = [P.tile(f"kct{g}", [64, S], BF16) for g in range(2)]
    vE = [P.tile(f"vE{g}", [128, 32, 128], BF16) for g in range(2)]
    vO = [P.tile(f"vO{g}", [128, 32, 128], BF16) for g in range(2)]
    onesE = P.tile("onesE", [128, 128], BF16)
    onesO = P.tile("onesO", [128, 128], BF16)
    wout = P.tile("wout", [128, 8, 1024], BF16)
    BT = P.tile("BT", [128, 2, 2, 2, 512], F32)
    esT = P.tile("esT", [128, 2, 512], F32)
    qs = [[P.tile(f"qs{g}{s}", [64, 2, 4, TT], BF16) for s in range(2)] for g in range(2)]
    tmp = [P.tile(f"tmp{i}", [128, 512], F32) for i in range(3)]
    pt = [P.tile(f"pt{i}", [128, 512], BF16) for i in range(4)]
    lsb = P.tile("lsb", [128, 512], F32)
    rl = P.tile("rl", [128, 512], F32)
    aoutT = [P.tile(f"aoT{g}", [128, 4, TT], BF16) for g in range(2)]
    xc = [P.tile(f"xc{i}", [128, TT], F32) for i in range(3)]

    P.op("pool", lambda e: e.memset(onesE[:, :], 0.0), w=[onesE])
    P.op("pool", lambda e: e.memset(onesE[:, 0:64], 1.0), w=[onesE])
    P.op("pool", lambda e: e.memset(onesO[:, :], 0.0), w=[onesO])
    P.op("pool", lambda e: e.memset(onesO[:, 64:128], 1.0), w=[onesO])
    P.op("sp", lambda e: e.dma_start(out=BT[:, :, :, :, :], in_=C.d_BT), w=[BT], dma=BT)
    P.op("sp", lambda e: e.dma_start(out=esT[:, :, :], in_=C.d_sinks[o_]), w=[esT], dma=esT)
    P.op("act", lambda e: e.activation(out=esT[:, :, :], in_=esT[:, :, :], func=AF.Exp), r=[esT], w=[esT])
    vcv = C.d_VC.rearrange("(c p) f -> p c f", p=128)
    for g in range(2):
        P.op("sp", lambda e, g=g: e.dma_start(out=kct[g][:, :], in_=C.d_KC[g * 64:(g + 1) * 64, :]), w=[kct[g]], dma=kct[g])
        P.op("pool", lambda e, g=g: e.memset(vE[g][:, :, 64:128], 0.0), w=[vE[g]])
        P.op("pool", lambda e, g=g: e.memset(vO[g][:, :, 0:64], 0.0), w=[vO[g]])
        P.op("sp", lambda e, g=g: e.dma_start(out=vE[g][:, :, 0:64], in_=vcv[:, :, g * 64:(g + 1) * 64]), w=[vE[g]], dma=vE[g])
        P.op("sp", lambda e, g=g: e.dma_start(out=vO[g][:, :, 64:128], in_=vcv[:, :, g * 64:(g + 1) * 64]), w=[vO[g]], dma=vO[g])
    P.op("pool", lambda e: e.dma_start(out=wout[:, :, :], in_=C.d_cwout[o_]), w=[wout], dma=wout)

    qkv = C.d_QK.rearrange("(g jj par d) s -> g d par jj s", g=2, jj=4, par=2, d=64)
    pti = 0
    xci = 0
    LA = 2
    sbanks = [ps[4], ps[5], ps[6]]
    for i in range(NT):
        t0 = i * TT
        s_ = i % 2
        for g in range(2):
            for par in range(2):
                P.op("sp", lambda e, g=g, par=par, t0=t0, s_=s_: e.dma_start(out=qs[g][s_][:, par, :, :], in_=qkv[g, :, par, :, t0:t0 + TT]),
                     w=[qs[g][s_]], dma=qs[g][s_])
        its = []
        for b in range(4):
            gb = 4 * i + b
            for g in range(2):
                chunks = [1] if gb == 0 else [0, 1]
                combos = [(ch, par) for ch in chunks for par in range(2)]
                for idx, (ch, par) in enumerate(combos):
                    its.append((b, g, ch, par, idx == 0, idx == len(combos) - 1))
        slot = {}

        def stageA(n):
            nonlocal pti
            b, g, ch, par, first, last = its[n]
            kb = 4 * i + b - 1 + ch
            sb = sbanks[pti % 3]
            p_ = pt[pti % 4]
            tm = tmp[pti % 3]
            pti += 1
            slot[n] = p_
            q_ = qs[g][s_]
            P.op("pe", lambda e: e.matmul(sb[:, :], lhsT=kct[g][:, kb * 128:(kb + 1) * 128],
                                          rhs=q_[:, par, :, b * 128:(b + 1) * 128], start=True, stop=True),
                 r=[kct[g], q_], w=[sb])
            P.op("dve", lambda e: e.tensor_tensor(out=tm[:, :], in0=sb[:, :], in1=BT[:, ch, g, par, :], op=ALU.add),
                 r=[sb, BT], w=[tm])
            P.op("act", lambda e: e.activation(out=p_[:, :], in_=tm[:, :], func=AF.Exp), r=[tm], w=[p_])

        def stageB(n):
            b, g, ch, par, first, last = its[n]
            kb = 4 * i + b - 1 + ch
            p_ = slot.pop(n)
            Ob, Lb = ps[0 + 2 * g], ps[1 + 2 * g]

            def mmo(e):
                vv = vE[g] if par == 0 else vO[g]
                oo = onesE if par == 0 else onesO
                e.matmul(Ob[:, :], lhsT=vv[:, kb, :], rhs=p_[:, :], start=first, stop=last)
                return e.matmul(Lb[:, :], lhsT=oo[:, :], rhs=p_[:, :], start=first, stop=last)
            P.op("pe", mmo, r=[vE[g], vO[g], onesE, onesO, p_], w=[Ob, Lb])
            if last:
                ao = aoutT[g]
                P.op("dve", lambda e: e.tensor_tensor(out=lsb[:, :], in0=Lb[:, :], in1=esT[:, g, :], op=ALU.add),
                     r=[Lb, esT], w=[lsb])
                P.op("dve", lambda e: e.reciprocal(out=rl[:, :], in_=lsb[:, :]), r=[lsb], w=[rl])
                P.op("dve", lambda e: e.tensor_tensor(
                    out=ao[:, :, b * 128:(b + 1) * 128], in0=Ob[:, :].rearrange("p (j q) -> p j q", j=4),
                    in1=rl[:, :].rearrange("p (j q) -> p j q", j=4), op=ALU.mult), r=[Ob, rl], w=[ao])

        nit = len(its)
        for n in range(nit + LA):
            if n < nit:
                stageA(n)
            if n - LA >= 0:
                stageB(n - LA)
        rhs_chunks = [aoutT[j // 4][:, j % 4, :] for j in range(8)]
        C._oproj_reads = aoutT
        xci = out_proj_tile(P, C, wout, rhs_chunks, Xin, Xout, t0, xc, xci, [ps[0], ps[2]])
    P.barrier()


def build_program(phases):
    nc = bass.Bass("TRN2", target_bir_lowering=False)
    C = Ctx()
    xT = nc.dram_tensor("xT", [D, S], F32, kind="ExternalInput").ap()
    out = nc.dram_tensor("out", [D, S], F32, kind="ExternalOutput").ap()
    XA = nc.dram_tensor("XA", [D, S], F32, kind="Internal").ap()
    XB = nc.dram_tensor("XB", [D, S], F32, kind="Internal").ap()
    C.d_wup = nc.dram_tensor("f_wup", [4, NFP, 128, 8, 256], F32, kind="ExternalInput").ap()
    C.d_wdn = nc.dram_tensor("f_wdn", [4, NFP // 2, 128, 2, 1024], F32, kind="ExternalInput").ap()
    C.d_fcpar = nc.dram_tensor("f_cpar", [4, 128, 44 * 4], F32, kind="ExternalInput").ap()
    C.d_fnorm = nc.dram_tensor("f_norm", [4, 128, 8], F32, kind="ExternalInput").ap()

    def din(name, shape):
        return nc.dram_tensor(name, list(shape), F32, kind="ExternalInput").ap()
    C.d_abnorm = din("ab_norm", [2, 128, 8])
    C.d_aqk = din("a_qk", [2, 128, 2])
    C.d_bscale = din("b_scale", [2, 128, 4])
    C.d_invc = din("invc", [128, 4, TT])
    C.d_abwin = din("ab_win", [2, 4, 128, 8, 512])
    C.d_wgrp = din("b_wgrp", [2, 128, 4, 128])
    C.d_biasA = din("biasA", [128, 4 * NDEL])
    C.d_alam = din("a_lam", [2, 128, 256])
    C.d_asub = din("a_sub", [2, 128, 1])
    C.d_mask01 = din("mask01", [128, 128])
    C.d_qrowsA = din("qrowsA", [4, 2, TT])
    C.d_abwout = din("ab_wout", [2, 128, 8, 1024])
    C.d_cnorm = din("c_norm", [2, 128, 8])
    C.d_cqk = din("c_qk", [2, 128, 2])
    C.d_cwin = din("c_win", [2, 5, 128, 8, 256])
    C.d_BT = din("BT", [128, 2, 2, 2, 512])
    C.d_sinks = din("c_sinks", [2, 128, 2, 512])
    C.d_cwout = din("c_wout", [2, 128, 8, 1024])
    C.d_QK = nc.dram_tensor("sQK", [1024, S], BF16, kind="Internal").ap()
    C.d_V = nc.dram_tensor("sV", [S, 512], BF16, kind="Internal").ap()
    C.d_MIXB = nc.dram_tensor("sMIXB", [512, S], BF16, kind="Internal").ap()
    C.d_KC = nc.dram_tensor("sKC", [128, S], BF16, kind="Internal").ap()
    C.d_VC = nc.dram_tensor("sVC", [S, 128], BF16, kind="Internal").ap()

    P = Plan(nc)
    pst = nc.alloc_psum_tensor("ps", [128, 8 * 512], F32)
    C.ps = []
    for b in range(8):
        tl = Tile(pst[:, b * 512:(b + 1) * 512], Buf(f"ps{b}"))
        C.ps.append(tl)

    Plan.SB_TOP = 229344 - 64
    epst = nc.alloc_sbuf_tensor_at("eps_col", [128, 1], F32, offset=Plan.SB_TOP)
    C.eps_col = Tile(epst, Buf("eps_col"))
    P.op("pool", lambda e: e.memset(C.eps_col[:, :], EPS), w=[C.eps_col])

    def fm(ap):
        return ap.rearrange("(c p) s -> p c s", p=128)

    n = len(phases)
    cur = xT
    for pi, (kind, l) in enumerate(phases):
        if pi == n - 1:
            dst = out
        else:
            dst = XA if (pi % 2 == 0) else XB
        if kind == "ffn":
            ffn_phase(P, C, l, fm(cur), fm(dst))
        elif kind == "even":
            even_proj_phase(P, C, l, fm(cur))
            even_attn_phase(P, C, l, fm(cur), fm(dst))
        elif kind == "odd":
            odd_proj_phase(P, C, l, fm(cur))
            odd_attn_phase(P, C, l, fm(cur), fm(dst))
        else:
            raise ValueError(kind)
        cur = dst
    P.emit()
    return nc, P


def prep_weights(inp):
    W = {}
    wu = np.asarray(inp["f_w_up"], dtype=np.float32)
    wu = wu.reshape(4, 8, 128, 2, NFP, 128)
    W["f_wup"] = np.ascontiguousarray(wu.transpose(0, 4, 2, 1, 3, 5)).reshape(4, NFP, 128, 8, 256)
    wd = np.asarray(inp["f_w_down"], dtype=np.float32)
    wd = wd.reshape(4, NFP // 2, 2, 128, 1024)
    W["f_wdn"] = np.ascontiguousarray(wd.transpose(0, 1, 3, 2, 4))
    fc = np.asarray(inp["f_conv"], dtype=np.float32)
    fb = np.asarray(inp["f_conv_b"], dtype=np.float32)
    cp = np.concatenate([fc, fb[:, None, :]], axis=1)
    cp = cp.reshape(4, 4, 44, 128).transpose(0, 3, 2, 1)
    W["f_cpar"] = np.ascontiguousarray(cp).reshape(4, 128, 44 * 4)
    fn = np.asarray(inp["f_norm"], dtype=np.float32).reshape(4, 8, 128).transpose(0, 2, 1)
    W["f_norm"] = np.ascontiguousarray(fn)
    f32 = np.float32

    def A(k):
        return np.asarray(inp[k], dtype=f32)

    def pc(w):
        n = w.shape[-1]
        return w.reshape(8, 128, n).transpose(1, 0, 2)
    W["ab_norm"] = np.ascontiguousarray(A("ab_norm").reshape(2, 8, 128).transpose(0, 2, 1))
    W["c_norm"] = np.ascontiguousarray(A("c_norm").reshape(2, 8, 128).transpose(0, 2, 1))
    W["a_qk"] = np.ascontiguousarray(np.stack([np.tile(A("a_q_norm"), (1, 2)), np.tile(A("a_k_norm"), (1, 2))], axis=-1))
    W["c_qk"] = np.ascontiguousarray(np.stack([np.tile(A("c_q_norm"), (1, 2)), np.tile(A("c_k_norm"), (1, 2))], axis=-1))
    W["b_scale"] = np.ascontiguousarray(A("b_scale").reshape(2, 4, 128).transpose(0, 2, 1))
    abw = A("ab_w_in")
    W["ab_win"] = np.ascontiguousarray(np.stack([np.stack([pc(abw[e][:, 512 * k:512 * (k + 1)]) for k in range(4)]) for e in range(2)]))
    W["b_wgrp"] = np.ascontiguousarray(A("b_w_group").transpose(0, 2, 1, 3))
    W["a_lam"] = np.ascontiguousarray(np.broadcast_to(A("a_lambda").reshape(2, 1, 256), (2, 128, 256)))
    W["a_sub"] = np.ascontiguousarray(A("a_sub_norm").reshape(2, 128, 1))
    W["ab_wout"] = np.ascontiguousarray(np.stack([pc(A("ab_w_out")[e]) for e in range(2)]))
    cw = A("c_w_in")
    W["c_win"] = np.ascontiguousarray(np.stack([np.stack([pc(cw[o][:, 256 * k:256 * (k + 1)]) for k in range(5)]) for o in range(2)]))
    W["c_wout"] = np.ascontiguousarray(np.stack([pc(A("c_w_out")[o]) for o in range(2)]))
    sk = A("c_sinks")
    sk = sk.reshape(2, 2, 4, 2).transpose(0, 3, 1, 2)
    sk = np.repeat(sk[:, :, None, :, :, None], 64, axis=2)
    sk = np.broadcast_to(sk, (2, 2, 64, 2, 4, 128)).reshape(2, 128, 2, 512)
    W["c_sinks"] = np.ascontiguousarray(sk)
    W.update(const_tables())
    return W


_CONST = {}


def const_tables():
    if _CONST:
        return _CONST
    f32 = np.float32
    kr = np.arange(128, dtype=np.float64)
    bA = np.zeros((128, 4, NDEL), dtype=np.float64)
    for h in range(4):
        for di in range(NDEL):
            bA[:, h, di] = A_SLOPES[h] * (kr - 128.0 * (di - 3))
    _CONST["biasA"] = bA.reshape(128, 4 * NDEL).astype(f32)
    qc = np.arange(TT)
    qr = np.zeros((4, 2, TT), dtype=np.float64)
    for h in range(4):
        qr[h, 0] = -A_SLOPES[h] * 128.0 * (qc // 128)
        qr[h, 1] = -A_SLOPES[h] * (qc % 128)
    _CONST["qrowsA"] = qr.astype(f32)
    _CONST["mask01"] = (np.arange(128)[None, :] >= np.arange(128)[:, None]).astype(f32)
    t = np.arange(TT)
    ic = np.zeros((128, 4, TT), dtype=f32)
    for g in range(4):
        ic[:, g, :] = (1.0 / np.minimum(t + 1, 2 ** (g + 1))).astype(f32)[None, :]
    _CONST["invc"] = ic
    BT = np.zeros((128, 2, 2, 2, 4, 128), dtype=np.float64)
    krr = np.arange(128)[:, None]
    qq = np.arange(128)[None, :]
    for g in range(2):
        for par in range(2):
            for jj in range(4):
                sl = C_SLOPES[8 * g + 2 * jj + par]
                rel_prev = 128 + qq - krr
                BT[:, 0, g, par, jj, :] = np.where(rel_prev < 128, -sl * rel_prev, -30000.0)
                rel_cur = qq - krr
                BT[:, 1, g, par, jj, :] = np.where(rel_cur >= 0, -sl * rel_cur, -30000.0)
    _CONST["BT"] = BT.reshape(128, 2, 2, 2, 512).astype(f32)
    return _CONST


ALL_PHASES = [("even", 0), ("ffn", 0), ("odd", 1), ("ffn", 1), ("even", 2), ("ffn", 2), ("odd", 3), ("ffn", 3)]

_CACHE = {}


def kernel(**inputs):
    x = np.asarray(inputs["x"], dtype=np.float32)
    B = x.shape[0]
    W = prep_weights(inputs)
    key = "full"
    if key not in _CACHE:
        _CACHE[key] = build_program(ALL_PHASES)[0]
    nc = _CACHE[key]
    in_maps = []
    for b in range(B):
        m = {"xT": np.ascontiguousarray(x[b].T)}
        m.update(W)
        in_maps.append(m)
    res = run_bass_kernel_spmd(nc, in_maps, core_ids=list(range(B)))
    outs = [np.asarray(r["out"]).T for r in res.results]
    return np.ascontiguousarray(np.stack(outs, axis=0)).astype(np.float32)
```
